# Optimizing a Trainium2 kernel written in Bass

```python
import jax, jax.numpy as jnp
from jax import lax
import numpy as np

D_MODEL = 2048
BATCH = 2
SEQ = 4096
DEPTH = 1

D_MIX = D_MODEL
D_POOL = D_MIX // 2
D_LRU = D_MIX - D_POOL
POOL_WINDOWS = (2, 4, 8, 16)
N_POOL_GROUPS = len(POOL_WINDOWS)
POOL_GROUP = D_POOL // N_POOL_GROUPS
N_LRU_HEADS = 8
LRU_HEAD = D_LRU // N_LRU_HEADS
CONV_WIDTH = 4
CONV_PAD = (1, 2)
RG_C = 8.0
D_FF = 4 * D_MODEL
LN_EPS = 1e-5
DEEPNORM_ALPHA = (2 * DEPTH) ** 0.25
DEEPNORM_BETA = (8 * DEPTH) ** -0.25

kernel_name = "hybrid_pool_rglru_deepnorm_encoder"


def layer_norm(x, g, b):
    xf = x.astype(jnp.float32)
    mu = jnp.mean(xf, axis=-1, keepdims=True)
    var = jnp.mean(jnp.square(xf - mu), axis=-1, keepdims=True)
    y = (xf - mu) * lax.rsqrt(var + LN_EPS) * g.astype(jnp.float32) + b.astype(jnp.float32)
    return y.astype(x.dtype)


def window_mean(u, w):
    S = u.shape[1]
    c = jnp.pad(jnp.cumsum(u.astype(jnp.float32), axis=1), ((0, 0), (1, 0), (0, 0)))
    t = jnp.arange(S)
    lo = jnp.clip(t - w // 2, 0, S)
    hi = jnp.clip(t + w // 2, 0, S)
    s = jnp.take(c, hi, axis=1) - jnp.take(c, lo, axis=1)
    cnt = (hi - lo).astype(jnp.float32)
    return s / cnt[None, :, None]


def multiscale_pool(u, w_pool, pool_scale):
    B, S, _ = u.shape
    ug = u.reshape(B, S, N_POOL_GROUPS, POOL_GROUP)
    means = jnp.stack([window_mean(ug[:, :, g], w) for g, w in enumerate(POOL_WINDOWS)], axis=2)
    d = (means - ug.astype(jnp.float32)).astype(u.dtype)
    out = jnp.einsum('bsgi,gio->bsgo', d, w_pool).reshape(B, S, D_POOL)
    return out * pool_scale


def depthwise_conv(u, conv_w, conv_b):
    y = lax.conv_general_dilated(u, conv_w, window_strides=(1,), padding=[CONV_PAD],
                                 dimension_numbers=('NWC', 'WIO', 'NWC'),
                                 feature_group_count=u.shape[-1])
    return y + conv_b


def _combine(c1, c2):
    a1, b1 = c1
    a2, b2 = c2
    return a1 * a2, a2 * b1 + b2


def linear_scan(a, b, reverse):
    return lax.associative_scan(_combine, (a, b), reverse=reverse, axis=1)[1]


def rg_lru_bidirectional(xc, w_a, b_a, w_i, b_i, lam):
    B, S, C = xc.shape
    xh = xc.reshape(B, S, N_LRU_HEADS, LRU_HEAD)
    pre_r = jnp.einsum('bshi,nhio->nbsho', xh, w_a).reshape(2, B, S, C)
    pre_i = jnp.einsum('bshi,nhio->nbsho', xh, w_i).reshape(2, B, S, C)
    r = jax.nn.sigmoid(pre_r.astype(jnp.float32) + b_a.astype(jnp.float32)[:, None, None, :])
    i = jax.nn.sigmoid(pre_i.astype(jnp.float32) + b_i.astype(jnp.float32)[:, None, None, :])
    log_a = -RG_C * r * jax.nn.softplus(-lam.astype(jnp.float32))[:, None, None, :]
    a = jnp.exp(log_a)
    mult = jnp.sqrt(-jnp.expm1(2.0 * log_a))
    b = mult * i * xc.astype(jnp.float32)[None]
    h_fwd = linear_scan(a[0], b[0], False)
    h_bwd = linear_scan(a[1], b[1], True)
    return (h_fwd + h_bwd).astype(xc.dtype)


def hybrid_mixer(x, w_in, w_pool, pool_scale, conv_w, conv_b,
                 w_rg_a, b_rg_a, w_rg_i, b_rg_i, rg_lambda, w_out):
    proj = jnp.einsum('bsd,de->bse', x, w_in)
    u_pool = proj[..., :D_POOL]
    u_rec = proj[..., D_POOL:D_POOL + D_LRU]
    u_gate = proj[..., D_POOL + D_LRU:]
    y_pool = multiscale_pool(u_pool, w_pool, pool_scale)
    xc = depthwise_conv(u_rec, conv_w, conv_b)
    h = rg_lru_bidirectional(xc, w_rg_a, b_rg_a, w_rg_i, b_rg_i, rg_lambda)
    y_rec = h * jax.nn.gelu(u_gate)
    y = jnp.concatenate([y_pool, y_rec], axis=-1)
    return jnp.einsum('bse,ed->bsd', y, w_out)


def squared_relu_mlp(x, w1, w2):
    h = jnp.square(jax.nn.relu(jnp.einsum('bsd,df->bsf', x, w1)))
    return jnp.einsum('bsf,fd->bsd', h, w2)


def setup_inputs(seed: int = 0) -> dict:
    key = jax.random.key(seed)
    ks = jax.random.split(key, 20)
    f32 = jnp.float32

    def nrm(k, shape, scale):
        return jax.random.normal(k, shape, f32) * scale

    x = jax.random.normal(ks[0], (BATCH, SEQ, D_MODEL), f32)
    ln_mix_g = 1.0 + nrm(ks[1], (DEPTH, D_MODEL), 0.02)
    ln_mix_b = nrm(ks[2], (DEPTH, D_MODEL), 0.02)
    w_in = nrm(ks[3], (DEPTH, D_MODEL, D_POOL + 2 * D_LRU), D_MODEL ** -0.5)
    w_pool = nrm(ks[4], (DEPTH, N_POOL_GROUPS, POOL_GROUP, POOL_GROUP), POOL_GROUP ** -0.5)
    pool_scale = 1.0 + nrm(ks[5], (DEPTH, D_POOL), 0.1)
    conv_w = nrm(ks[6], (DEPTH, CONV_WIDTH, 1, D_LRU), CONV_WIDTH ** -0.5)
    conv_b = nrm(ks[7], (DEPTH, D_LRU), 0.02)
    w_rg_a = nrm(ks[8], (DEPTH, 2, N_LRU_HEADS, LRU_HEAD, LRU_HEAD), LRU_HEAD ** -0.5)
    b_rg_a = nrm(ks[9], (DEPTH, 2, D_LRU), 0.02)
    w_rg_i = nrm(ks[10], (DEPTH, 2, N_LRU_HEADS, LRU_HEAD, LRU_HEAD), LRU_HEAD ** -0.5)
    b_rg_i = nrm(ks[11], (DEPTH, 2, D_LRU), 0.02)
    u = jax.random.uniform(ks[12], (DEPTH, 2, D_LRU), f32, minval=0.9, maxval=0.999)
    s = u ** (1.0 / RG_C)
    rg_lambda = jnp.log(s) - jnp.log1p(-s)
    w_out = nrm(ks[13], (DEPTH, D_MIX, D_MODEL), D_MIX ** -0.5 * DEEPNORM_BETA)
    ln_ffn_g = 1.0 + nrm(ks[14], (DEPTH, D_MODEL), 0.02)
    ln_ffn_b = nrm(ks[15], (DEPTH, D_MODEL), 0.02)
    w_mlp_in = nrm(ks[16], (DEPTH, D_MODEL, D_FF), D_MODEL ** -0.5 * DEEPNORM_BETA)
    w_mlp_out = nrm(ks[17], (DEPTH, D_FF, D_MODEL), D_FF ** -0.5 * DEEPNORM_BETA)
    return {"x": x, "ln_mix_g": ln_mix_g, "ln_mix_b": ln_mix_b, "w_in": w_in,
            "w_pool": w_pool, "pool_scale": pool_scale, "conv_w": conv_w, "conv_b": conv_b,
            "w_rg_a": w_rg_a, "b_rg_a": b_rg_a, "w_rg_i": w_rg_i, "b_rg_i": b_rg_i,
            "rg_lambda": rg_lambda, "w_out": w_out, "ln_ffn_g": ln_ffn_g, "ln_ffn_b": ln_ffn_b,
            "w_mlp_in": w_mlp_in, "w_mlp_out": w_mlp_out}


def reference(x, ln_mix_g, ln_mix_b, w_in, w_pool, pool_scale, conv_w, conv_b,
              w_rg_a, b_rg_a, w_rg_i, b_rg_i, rg_lambda, w_out, ln_ffn_g, ln_ffn_b,
              w_mlp_in, w_mlp_out):
    for l in range(DEPTH):
        mix = hybrid_mixer(x, w_in[l], w_pool[l], pool_scale[l], conv_w[l], conv_b[l],
                           w_rg_a[l], b_rg_a[l], w_rg_i[l], b_rg_i[l], rg_lambda[l], w_out[l])
        x = layer_norm(DEEPNORM_ALPHA * x + mix, ln_mix_g[l], ln_mix_b[l])
        ffn = squared_relu_mlp(x, w_mlp_in[l], w_mlp_out[l])
        x = layer_norm(DEEPNORM_ALPHA * x + ffn, ln_ffn_g[l], ln_ffn_b[l])
    return x
```

```python
import numpy as np
from contextlib import ExitStack
import concourse.bass as bass
import concourse.mybir as mybir
from concourse.bass_utils import run_bass_kernel_spmd

F32, BF16 = mybir.dt.float32, mybir.dt.bfloat16
AF = mybir.ActivationFunctionType
ALU = mybir.AluOpType

D = 2048
S = 4096
T = 1024
NPASS = 5
TS = 1028
TO = 1040
DFF = 8192
ALPHA = 2.0 ** 0.25
EPS = 1e-5
WINDOWS = (2, 4, 8, 16)
WBLK = 256
NSLOT = 4

VEC = {}
_off = 0
def _v(name, n):
    global _off
    VEC[name] = _off
    _off += n
for _p in range(NPASS):
    for _k in range(4):
        _v(f"tap{_p}_{_k}", 8)
    _v(f"cb{_p}", 8); _v(f"ba{_p}", 8); _v(f"bi{_p}", 8); _v(f"lam{_p}", 8)
_v("keep1", 1); _v("keep2", 1)
for _j in range(3):
    _v(f"selF{_j}", 1)
for _j in range(3):
    _v(f"selB{_j}", 1)
_v("ps", 8); _v("g1", 16); _v("b1", 16); _v("g2", 16); _v("b2", 16)
_v("invc", 64)
NV = _off


class Tr:
    ENG = ["pe", "act", "dve", "pool", "sp"]

    def __init__(self, nc, es):
        self.nc, self.es = nc, es
        self.semh = {"E" + e: es.enter_context(nc.semaphore("s_" + e)) for e in self.ENG}
        self.cnt = {e: 0 for e in self.ENG}
        self.stream = {e: [] for e in self.ENG}
        self.res = {}
        self.waited = {e: {} for e in self.ENG}
        self.dcnt = {}

    def _deps(self, eng, reads, writes, skip_self=False):
        deps = {}

        def add(s, v):
            if deps.get(s, 0) < v:
                deps[s] = v
        for k in reads:
            r = self.res.get(k)
            if r and r["w"]:
                add(*r["w"])
        for k in writes:
            r = self.res.get(k)
            if r:
                if r["w"]:
                    add(*r["w"])
                for s, v in r["r"].items():
                    add(s, v)
        out = []
        for s, v in deps.items():
            if skip_self and s == "E" + eng:
                continue
            if self.waited[eng].get(s, 0) >= v:
                continue
            self.waited[eng][s] = v
            out.append((s, v))
        return out

    def _mark(self, reads, writes, sv):
        for k in reads:
            r = self.res.setdefault(k, {"w": None, "r": {}})
            if r["r"].get(sv[0], 0) < sv[1]:
                r["r"][sv[0]] = sv[1]
        for k in writes:
            self.res[k] = {"w": sv, "r": {}}

    def group(self, eng, fns, reads=(), writes=(), skip_self=False):
        deps = self._deps(eng, reads, writes, skip_self)
        self.cnt[eng] += 1
        sv = ("E" + eng, self.cnt[eng])
        self.stream[eng].append((deps, list(fns), (sv[0], 1)))
        self._mark(reads, writes, sv)

    def op(self, eng, fn, reads=(), writes=()):
        self.group(eng, [fn], reads, writes)

    def dma(self, eng, fn, sem, reads=(), writes=()):
        key = "D" + sem
        if key not in self.semh:
            self.semh[key] = self.es.enter_context(self.nc.semaphore("d_" + sem))
            self.dcnt[key] = 0
        deps = self._deps(eng, reads, writes)
        self.dcnt[key] += 16
        sv = (key, self.dcnt[key])
        self.stream[eng].append((deps, [fn], (key, 16)))
        self._mark(reads, writes, sv)

    def barrier(self):
        for e in self.ENG:
            deps = []
            allv = [("E" + e2, self.cnt[e2]) for e2 in self.ENG] + list(self.dcnt.items())
            for s, v in allv:
                if v > 0 and self.waited[e].get(s, 0) < v:
                    self.waited[e][s] = v
                    deps.append((s, v))
            self.stream[e].append((deps, [], None))
        self.res = {}

    def wait_all(self, e):
        deps = []
        allv = [("E" + e2, self.cnt[e2]) for e2 in self.ENG] + list(self.dcnt.items())
        for s, v in allv:
            if v > 0 and self.waited[e].get(s, 0) < v:
                self.waited[e][s] = v
                deps.append((s, v))
        self.stream[e].append((deps, [], None))

    def emit(self):
        nc = self.nc
        with nc.Block() as block:
            def mk(e):
                def run(eng):
                    for deps, fns, inc in self.stream[e]:
                        for s, v in deps:
                            eng.wait_ge(self.semh[s], v)
                        ins = None
                        for f in fns:
                            ins = f(eng)
                        if inc is not None and ins is not None:
                            ins.then_inc(self.semh[inc[0]], inc[1])
                return run
            block.tensor(mk("pe"))
            block.scalar(mk("act"))
            block.vector(mk("dve"))
            block.gpsimd(mk("pool"))
            block.sync(mk("sp"))


def build_nc(debug=None):
    nc = bass.Bass("TRN2", target_bir_lowering=False)
    xs_d = nc.dram_tensor("xs", [4, D, TS], F32, kind="ExternalInput").ap()
    xo_d = nc.dram_tensor("xo", [D, TO], F32, kind="ExternalInput").ap()
    win_d = nc.dram_tensor("w_in", [D, 3072], F32, kind="ExternalInput").ap()
    wout_d = nc.dram_tensor("w_out", [D, D], F32, kind="ExternalInput").ap()
    w1_d = nc.dram_tensor("w1", [D, DFF], F32, kind="ExternalInput").ap()
    w2_d = nc.dram_tensor("w2", [DFF, D], F32, kind="ExternalInput").ap()
    wp_d = nc.dram_tensor("wp", [4, 256, 256], F32, kind="ExternalInput").ap()
    wg_d = nc.dram_tensor("wg", [NPASS, 2, 8, 128, 128], F32, kind="ExternalInput").ap()
    vec_d = nc.dram_tensor("vec", [128, NV], F32, kind="ExternalInput").ap()
    out_d = nc.dram_tensor("out", [D, T], F32, kind="ExternalOutput").ap()

    es = ExitStack()
    with es:
        def sb(name, shape, dt):
            return es.enter_context(nc.sbuf_tensor(name, shape, dt))
        XW = 16 * TO // 2
        RX = sb("rx", [128, 2 * XW], F32)
        RHS = sb("rhs", [128, 8192], F32)
        RTMP = sb("rtmp", [128, 14848], F32)
        wbuf = sb("wbuf", [128, NSLOT, 16, WBLK], BF16)
        gw = sb("gw", [128, 3, 2, 8, 128], BF16)
        wp = sb("wpool", [128, 4, 2, 256], BF16)
        vec = sb("vec_sb", [128, NV], F32)
        dv = sb("dv", [128, NPASS, 4, 8], F32)
        sc = sb("sc", [128, 8, 40], F32)
        est = sb("est", [128, 3, 8], F32)
        ini = sb("ini", [128, 4, 8], F32)
        ones = sb("ones", [128, 128], BF16)
        ps = es.enter_context(nc.psum_tensor("psum_all", [128, 8, 512], F32))

        def view(raw, off, n, dt, pat=None, **kw):
            v = raw[:, off:off + n]
            if dt != F32:
                v = v.bitcast(dt)
            if pat:
                v = v.rearrange(pat, **kw)
            return v
        XT = [view(RX, i * XW, XW, BF16, "p (k n) -> p k n", k=16) for i in range(2)]
        Z = view(RX, 0, 16384, F32, "p (k n) -> p k n", k=16)
        HS = view(RHS, 0, 8192, F32, "p (k n) -> p k n", k=8)
        HQ = view(RHS, 0, 8192, BF16, "p (k n) -> p k n", k=16)
        LNT = [view(RHS, i * 1024, 1024, F32) for i in range(4)]
        ZB = [view(RHS, 4096 + i * 512, 512, BF16) for i in range(2)]
        ZQ = [view(RHS, 5120 + i * 512, 512, BF16) for i in range(2)]
        o = 0
        XC = [view(RTMP, o + i * 1024, 1024, F32) for i in range(3)]; o += 3072
        XCB = [view(RTMP, o + i * 512, 512, BF16) for i in range(3)]; o += 1536
        TR = view(RTMP, o, 1024, F32); o += 1024
        TI = [view(RTMP, o + i * 1024, 1024, F32) for i in range(3)]; o += 3072
        AA = [view(RTMP, o + i * 1024, 1024, F32) for i in range(3)]; o += 3072
        A2 = [view(RTMP, o + i * 1024, 1024, F32) for i in range(3)]; o += 3072
        assert o <= 14848
        Y = view(RTMP, 0, 8192, BF16, "p (k n) -> p k n", k=16)
        GL = [view(RTMP, 8192 + i * 1024, 1024, F32) for i in range(2)]
        RL = [view(RTMP, 8192 + 2048 + i * 512, 512, F32) for i in range(2)]
        UP = [view(RX, i * 1040, 1040, F32) for i in range(2)]
        PT = [view(RX, 2080 + i * 1040, 1040, F32) for i in range(2)]
        DP = view(RX, 4160, 4096, BF16, "p (k n) -> p k n", k=8)
        T8 = view(RX, 8256, 8, F32)
        assert 8264 <= XW

        tr = Tr(nc, es)

        def vcol(name, i=0, n=1):
            return vec[:, VEC[name] + i: VEC[name] + i + n]

        blocks = []
        win_v = win_d.rearrange("(k p) n -> p k n", p=128)
        wout_v = wout_d.rearrange("(k p) n -> p k n", p=128)
        w1_v = w1_d.rearrange("(k p) n -> p k n", p=128)
        w2_v = w2_d.rearrange("(q k p) n -> q p k n", p=128, k=16)
        for p in range(4):
            for j in range(4):
                blocks.append(win_v[:, :, 1024 + j * WBLK:1024 + (j + 1) * WBLK])
        B_POOL = len(blocks)
        for j in range(4):
            blocks.append(win_v[:, :, j * WBLK:(j + 1) * WBLK])
        B_GATE = len(blocks)
        for j in range(4):
            blocks.append(win_v[:, :, 2048 + j * WBLK:2048 + (j + 1) * WBLK])
        B_OUT = len(blocks)
        for j in range(8):
            blocks.append(wout_v[:, :, j * WBLK:(j + 1) * WBLK])
        B_MLP = len(blocks)
        for q in range(4):
            for j in range(8):
                blocks.append(w1_v[:, :, q * 2048 + j * WBLK:q * 2048 + (j + 1) * WBLK])
            for j in range(8):
                blocks.append(w2_v[q][:, :, j * WBLK:(j + 1) * WBLK])
        issued = [0]

        def need(bi, look=3):
            while issued[0] < len(blocks) and issued[0] <= bi + look:
                b = issued[0]
                s = b % NSLOT
                src = blocks[b]
                tr.dma("pool", lambda e, s=s, src=src: e.dma_start(out=wbuf[:, s], in_=src),
                       f"w{s}", reads=(), writes=(f"wb{s}",))
                issued[0] += 1
            return bi % NSLOT

        def mm_group(bank_cols, lhs_fn, rhs_fn, nk, reads, writes, k0=0, k1=None, skip_self=False):
            fns = []
            for k in range(k0, nk if k1 is None else k1):
                for (pap, rarg) in bank_cols:
                    l, r = lhs_fn(k), rhs_fn(k, rarg)
                    fns.append(lambda e, pap=pap, l=l, r=r, k=k: e.matmul(
                        pap, l, r, start=(k == 0), stop=(k == nk - 1)))
            tr.group("pe", fns, reads, writes, skip_self=skip_self)

        tr.dma("sp", lambda e: e.dma_start(out=vec[:], in_=vec_d), "vec", writes=("vec",))
        tr.op("dve", lambda e: e.memset(ones[:], 1.0), writes=("ones",))
        tr.dma("pool", lambda e: e.dma_start(out=wp[:], in_=wp_d.rearrange("g (c i) o -> i g c o", i=128)),
               "wp", writes=("wp",))
        lamv = vec[:, VEC["lam0"]:VEC["lam0"] + 8]
        for p in range(NPASS):
            lam = vec[:, VEC[f"lam{p}"]:VEC[f"lam{p}"] + 8]
            ee, zz, z2, acc, tmp = (sc[:, i, p * 8:(p + 1) * 8] for i in range(5))
            tr.op("act", lambda e, ee=ee, lam=lam: e.activation(ee, lam, AF.Exp, scale=-1.0),
                  reads=("vec",), writes=(f"sc{p}",))
            tr.op("dve", lambda e, ee=ee, tmp=tmp: e.tensor_scalar_add(tmp, ee, 2.0), reads=(f"sc{p}",), writes=(f"sc{p}",))
            tr.op("dve", lambda e, tmp=tmp: e.reciprocal(tmp, tmp), reads=(f"sc{p}",), writes=(f"sc{p}",))
            tr.op("dve", lambda e, zz=zz, ee=ee, tmp=tmp: e.tensor_mul(zz, ee, tmp), reads=(f"sc{p}",), writes=(f"sc{p}",))
            tr.op("dve", lambda e, zz=zz, z2=z2: e.tensor_mul(z2, zz, zz), reads=(f"sc{p}",), writes=(f"sc{p}",))
            tr.op("dve", lambda e, acc=acc, z2=z2: e.tensor_scalar(acc, z2, 1.0 / 11, 1.0 / 9, ALU.mult, ALU.add),
                  reads=(f"sc{p}",), writes=(f"sc{p}",))
            for cst in (1.0 / 7, 1.0 / 5, 1.0 / 3, 1.0):
                tr.op("dve", lambda e, acc=acc, z2=z2: e.tensor_mul(acc, acc, z2), reads=(f"sc{p}",), writes=(f"sc{p}",))
                tr.op("dve", lambda e, acc=acc, cst=cst: e.tensor_scalar_add(acc, acc, cst), reads=(f"sc{p}",), writes=(f"sc{p}",))
            tr.op("dve", lambda e, acc=acc, zz=zz: e.tensor_mul(acc, acc, zz), reads=(f"sc{p}",), writes=(f"sc{p}",))
            tr.op("dve", lambda e, acc=acc, p=p: e.tensor_scalar_mul(dv[:, p, 0, :], acc, -8.0), reads=(f"sc{p}",), writes=(f"dv{p}",))
            tr.op("dve", lambda e, acc=acc, p=p: e.tensor_scalar_mul(dv[:, p, 1, :], acc, -16.0), reads=(f"sc{p}",), writes=(f"dv{p}",))
            tr.op("dve", lambda e, p=p: e.tensor_scalar_mul(dv[:, p, 2, :], vec[:, VEC[f"ba{p}"]:VEC[f"ba{p}"] + 8], 0.5),
                  reads=("vec",), writes=(f"dv{p}",))
            tr.op("dve", lambda e, p=p: e.tensor_scalar_mul(dv[:, p, 3, :], vec[:, VEC[f"bi{p}"]:VEC[f"bi{p}"] + 8], 0.5),
                  reads=("vec",), writes=(f"dv{p}",))

        xs_v = xs_d.rearrange("s (k p) n -> s p k n", p=128)
        xo_v = xo_d.rearrange("(k p) n -> p k n", p=128)
        wg_v = wg_d.rearrange("s g h i o -> s i g h o")

        GWI = {0: 0, 1: 1, 2: 0, 3: 1, 4: 2}

        def load_pass_inputs(p):
            if p < 4:
                xt = XT[p % 2]
                tr.dma("pool", lambda e, p=p, xt=xt: e.dma_start(out=xt[:, :, 0:TS], in_=xs_v[p]),
                       f"x{p % 2}", writes=(f"xt{p % 2}",))
            tr.dma("pool", lambda e, p=p: e.dma_start(out=gw[:, GWI[p]], in_=wg_v[p]),
                   f"gw{GWI[p]}", writes=(f"gw{GWI[p]}",))

        need(0, look=0)
        load_pass_inputs(0)
        tr.op("pool", lambda e: e.memset(sc[:, 7, 0:8], 0.0), reads=("xt0", "wb0", "gw0"), writes=("scpad",))
        need(0, look=1)
        load_pass_inputs(1)
        load_pass_inputs(4)
        units = [(p, m) for p in range(3) for m in range(8)]
        for m in range(8):
            units += [(3, m), (4, m)]
        FULL, XI, FC = [], [], []
        fc = 0
        for (p, m) in units:
            if p < 4:
                FULL.append(True); XI.append(fc % 3); FC.append(fc); fc += 1
            else:
                FULL.append(False); XI.append(XI[-1]); FC.append(FC[-1])
        NU = len(units)
        TT = [(0, 512), (512, 512), (1024, 4)]

        def e_inproj(ci, part):
            p, m = units[ci]
            xt = XT[p % 2]
            bi = p * 4 + m // 2
            s = need(bi) if part == 0 else bi % NSLOT
            cs = (m % 2) * 128
            ub = 3 * (FC[ci] % 2)
            ukeys = tuple(f"ps{ub + tt}" for tt in range(3))
            mm_group([(ps[:, ub + tt, 0:n], (c0, n)) for tt, (c0, n) in enumerate(TT)],
                     lambda k: wbuf[:, s, k, cs:cs + 128],
                     lambda k, a: xt[:, k, a[0]:a[0] + a[1]], 16,
                     reads=(f"wb{s}", f"xt{p % 2}"), writes=ukeys,
                     k0=0 if part == 0 else 3, k1=3 if part == 0 else 16, skip_self=(part == 1))
            if part == 1:
                if m == 1 and p >= 1 and p + 1 < 4:
                    load_pass_inputs(p + 1)
                if m == 7 and p == 3:
                    tr.dma("pool", lambda e: e.dma_start(out=XT[1], in_=xo_v), "x1", writes=("xt1",))

        def conv_args(ci):
            p, m = units[ci]
            ub = 3 * (FC[ci] % 2)
            ukeys = tuple(f"ps{ub + tt}" for tt in range(3))
            tap = lambda k: vec[:, VEC[f"tap{p}_{k}"] + m:VEC[f"tap{p}_{k}"] + m + 1]
            cb = vec[:, VEC[f"cb{p}"] + m:VEC[f"cb{p}"] + m + 1]
            return ub, ukeys, tap, cb

        def e_tap0(ci):
            ub, ukeys, tap, cb = conv_args(ci)
            x3 = XC[XI[ci]].rearrange("p (a b) -> p a b", a=2)
            tr.op("act", lambda e: e.activation(x3, ps[:, ub:ub + 2, :], AF.Identity, bias=cb, scale=tap(0)),
                  reads=ukeys[:2] + ("vec",), writes=(f"xc{XI[ci]}",))

        def e_taps(ci):
            ub, ukeys, tap, cb = conv_args(ci)
            x = XC[XI[ci]]
            uflat = ps[:, ub:ub + 3, :].rearrange("p a b -> p (a b)")
            for k in (1, 2, 3):
                tr.op("dve", lambda e, k=k: e.scalar_tensor_tensor(x, uflat[:, k:k + 1024], tap(k), x, ALU.mult, ALU.add),
                      reads=ukeys + (f"xc{XI[ci]}", "vec"), writes=(f"xc{XI[ci]}",))

        def e_cast(ci):
            tr.op("act", lambda e: e.activation(XCB[XI[ci]], XC[XI[ci]], AF.Copy), reads=(f"xc{XI[ci]}",), writes=(f"xcb{XI[ci]}",))

        def e_gate(ci, g):
            p, m = units[ci]
            j = ci % 3
            xi = XI[ci]
            fns = [lambda e, h=h: e.matmul(ps[:, 6 + h, :], gw[:, GWI[p], g, m, :],
                                           XCB[xi][:, h * 512:(h + 1) * 512], start=True, stop=True) for h in range(2)]
            tr.group("pe", fns, reads=(f"xcb{xi}", f"gw{GWI[p]}"), writes=("ps6", "ps7"))

        def e_tanh(ci, g):
            p, m = units[ci]
            j = ci % 3
            dstt, dkey = (TR, "tr") if g == 0 else (TI[j], f"ti{j}")
            hb = dv[:, p, 2 + g, m:m + 1]
            d3 = dstt.rearrange("p (a b) -> p a b", a=2)
            tr.op("act", lambda e: e.activation(d3, ps[:, 6:8, :], AF.Tanh, bias=hb, scale=0.5),
                  reads=("ps6", "ps7", f"dv{p}"), writes=(dkey,))

        def e_exps(ci):
            p, m = units[ci]
            j = ci % 3
            hc, cc = dv[:, p, 0, m:m + 1], dv[:, p, 1, m:m + 1]
            tr.op("act", lambda e: e.activation(AA[j], TR, AF.Exp, bias=hc, scale=hc), reads=("tr", f"dv{p}"), writes=(f"aa{j}",))
            tr.op("act", lambda e: e.activation(A2[j], TR, AF.Exp, bias=cc, scale=cc), reads=("tr", f"dv{p}"), writes=(f"a2{j}",))

        def e_D2(ci):
            j = ci % 3
            xi = XI[ci]
            tr.op("dve", lambda e: e.scalar_tensor_tensor(TI[j], TI[j], 1.0, XC[xi], ALU.add, ALU.mult),
                  reads=(f"ti{j}", f"xc{xi}"), writes=(f"ti{j}",))

        def stageA7(ci):
            j = ci % 3
            tr.op("act", lambda e: e.activation(A2[j], A2[j], AF.Sqrt, bias=1.0, scale=-1.0), reads=(f"a2{j}",), writes=(f"a2{j}",))

        def stageB2(ci):
            p, m = units[ci]
            j = ci % 3
            tr.op("dve", lambda e: e.scalar_tensor_tensor(TI[j], TI[j], 0.5, A2[j], ALU.mult, ALU.mult),
                  reads=(f"ti{j}", f"a2{j}"), writes=(f"ti{j}",))
            if p == 0:
                init, ird = 0.0, ()
            else:
                init, ird = ini[:, p - 1, m:m + 1], ("ini",)
            if p == 4:
                tr.op("dve", lambda e: e.tensor_tensor_scan(TI[j][:, ::-1], AA[j][:, ::-1], TI[j][:, ::-1], init, ALU.mult, ALU.add),
                      reads=(f"aa{j}", f"ti{j}") + ird, writes=(f"ti{j}",))
                tr.op("dve", lambda e: e.tensor_add(HS[:, m, :], HS[:, m, :], TI[j]), reads=(f"ti{j}", f"hs{m}"), writes=(f"hs{m}",))
            elif p == 3:
                tr.op("dve", lambda e: e.tensor_tensor_scan(HS[:, m, :], AA[j], TI[j], init, ALU.mult, ALU.add),
                      reads=(f"aa{j}", f"ti{j}") + ird, writes=(f"hs{m}",))
            else:
                tr.op("dve", lambda e: e.tensor_tensor_scan(TI[j], AA[j], TI[j], init, ALU.mult, ALU.add),
                      reads=(f"aa{j}", f"ti{j}") + ird, writes=(f"ti{j}",))
                tr.op("dve", lambda e: e.tensor_copy(est[:, p, m:m + 1], TI[j][:, 1023:1024]), reads=(f"ti{j}",), writes=("est",))
            if m == 7 and p < 2:
                kp = vcol(f"keep{p + 1}")
                tr.op("dve", lambda e: e.tensor_scalar_mul(ini[:, p, :], est[:, p, :], kp), reads=("est", "vec"), writes=("ini",))
            if m == 7 and p == 2:
                for q, nm in ((2, "selF"), (3, "selB")):
                    tr.op("dve", lambda e, q=q, nm=nm: e.tensor_scalar_mul(ini[:, q, :], est[:, 0, :], vcol(nm + "0")),
                          reads=("est", "vec"), writes=("ini",))
                    for jj in (1, 2):
                        tr.op("dve", lambda e, q=q, nm=nm, jj=jj: e.scalar_tensor_tensor(
                            ini[:, q, :], est[:, jj, :], vcol(nm + str(jj)), ini[:, q, :], ALU.mult, ALU.add),
                            reads=("est", "vec", "ini"), writes=("ini",))

        for t in range(NU + 5):
            u2, u1, uf = t - 2, t - 1, t - 5
            v2, v1 = 0 <= u2 < NU, 0 <= u1 < NU
            if 0 <= uf < NU:
                stageB2(uf)
            if v2:
                e_gate(u2, 0)
            if t < NU and FULL[t]:
                e_inproj(t, 0)
            if v2:
                e_tanh(u2, 0)
            if v1 and FULL[u1]:
                e_tap0(u1)
                e_taps(u1)
            if v2:
                e_gate(u2, 1)
            if t < NU and FULL[t]:
                e_inproj(t, 1)
            if v2:
                e_tanh(u2, 1)
                e_exps(u2)
                e_D2(u2)
            if v1 and FULL[u1]:
                e_cast(u1)
            if v2 and u2 % 2 == 1:
                stageA7(u2 - 1); stageA7(u2)

        tr.barrier()
        out_v = out_d.rearrange("(k p) n -> p k n", p=128)

        def dump_and_finish(src_ap, k0, nk):
            tr.barrier()
            tr.dma("sp", lambda e: e.dma_start(out=out_v[:, k0:k0 + nk, :], in_=src_ap), "o0", reads=(), writes=())
            tr.barrier()
            tr.emit()
        if debug == "hs":
            dump_and_finish(HS, 0, 8)
            return nc

        xo = XT[1]
        TTO = [(0, 512), (512, 512), (1024, 16)]
        def phaseP_chunk(m):
            g = m // 2
            w = WINDOWS[g]
            s = need(B_POOL + m // 2)
            cs = (m % 2) * 128
            bb = 3 * (m % 2)
            mm_group([(ps[:, bb + tt, 0:n], (c0, n)) for tt, (c0, n) in enumerate(TTO)],
                     lambda k: wbuf[:, s, k, cs:cs + 128],
                     lambda k, a: xo[:, k, a[0]:a[0] + a[1]], 16,
                     reads=(f"wb{s}", "xt1"), writes=tuple(f"ps{bb + tt}" for tt in range(3)))
            up = UP[m % 2]
            for tt, (c0, n) in enumerate(TTO):
                tr.op("act", lambda e, tt=tt, c0=c0, n=n, up=up, bb=bb: e.activation(up[:, c0:c0 + n], ps[:, bb + tt, 0:n], AF.Copy),
                      reads=(f"ps{bb + tt}",), writes=(f"up{m % 2}",))
            src, L, step, pi = up, 1040, 1, 0
            while step < w:
                dst = PT[pi]
                n = L - step
                tr.op("dve", lambda e, dst=dst, src=src, n=n, step=step: e.tensor_add(dst[:, 0:n], src[:, 0:n], src[:, step:step + n]),
                      reads=(f"up{m % 2}", "pt0", "pt1"), writes=(f"pt{pi}",))
                src, L, step, pi = dst, n, step * 2, 1 - pi
            st0 = 8 - w // 2
            W = src
            tr.op("dve", lambda e, W=W, up=up: e.scalar_tensor_tensor(DP[:, m, :], W[:, st0:st0 + 1024], 1.0 / w, up[:, 8:1032],
                                                                      ALU.mult, ALU.subtract),
                  reads=("pt0", "pt1", f"up{m % 2}"), writes=(f"dp{m}",))
            for (c0, e0) in ((0, 0), (1016, 8)):
                iv = vec[:, VEC["invc"] + g * 16 + e0:VEC["invc"] + g * 16 + e0 + 8]
                tr.op("dve", lambda e, W=W, c0=c0, iv=iv: e.tensor_mul(T8, W[:, st0 + c0:st0 + c0 + 8], iv),
                      reads=("pt0", "pt1", "vec"), writes=("t8",))
                tr.op("dve", lambda e, c0=c0, up=up: e.tensor_sub(DP[:, m, c0:c0 + 8], T8, up[:, 8 + c0:16 + c0]),
                      reads=("t8", f"up{m % 2}", f"dp{m}"), writes=(f"dp{m}",))
            if m % 2 == 1:
                for oc in range(2):
                    for h in range(2):
                        bk = 6 + h
                        fns = [lambda e, ic=ic, oc=oc, h=h, bk=bk: e.matmul(ps[:, bk, :], wp[:, g, ic, oc * 128:(oc + 1) * 128],
                                                                           DP[:, 2 * g + ic, h * 512:(h + 1) * 512],
                                                                           start=(ic == 0), stop=(ic == 1)) for ic in range(2)]
                        tr.group("pe", fns, reads=("wp", f"dp{2 * g}", f"dp{2 * g + 1}"), writes=(f"ps{bk}",))
                        mo = 2 * g + oc
                        tr.op("act", lambda e, mo=mo, h=h, bk=bk: e.activation(Y[:, mo, h * 512:(h + 1) * 512], ps[:, bk, :], AF.Identity,
                                                                                scale=vcol("ps", mo)),
                              reads=(f"ps{bk}", "vec"), writes=(f"y{mo}",))

        for m in range(8):
            phaseP_chunk(m)
        out_v = out_d.rearrange("(k p) n -> p k n", p=128)
        xo_f = xo_d.rearrange("(k p) n -> p k n", p=128)

        def load_xres(q4):
            tr.dma("sp", lambda e: e.dma_start(out=Z[:, 4 * q4:4 * q4 + 4, :], in_=xo_f[:, 4 * q4:4 * q4 + 4, 8:1032]),
                   f"xr{q4}", writes=tuple(f"z{4 * q4 + i}" for i in range(4)))
        tr.wait_all("sp")
        load_xres(0)
        load_xres(1)

        def phaseG_chunk(m):
            s = need(B_GATE + m // 2)
            cs = (m % 2) * 128
            bb = 2 * (m % 2)
            mm_group([(ps[:, bb + h, :], (8 + h * 512, 512)) for h in range(2)],
                     lambda k: wbuf[:, s, k, cs:cs + 128],
                     lambda k, a: xo[:, k, a[0]:a[0] + a[1]], 16,
                     reads=(f"wb{s}", "xt1"), writes=(f"ps{bb}", f"ps{bb + 1}"))
            gl = GL[m % 2]
            for h in range(2):
                tr.op("act", lambda e, h=h, gl=gl, bb=bb: e.activation(gl[:, h * 512:(h + 1) * 512], ps[:, bb + h, :], AF.Gelu_apprx_tanh),
                      reads=(f"ps{bb + h}",), writes=(f"gl{m % 2}",))
            tr.op("dve", lambda e, gl=gl: e.tensor_mul(Y[:, 8 + m, :], HS[:, m, :], gl),
                  reads=(f"gl{m % 2}", f"hs{m}"), writes=(f"y{8 + m}",))
        for m in range(8):
            phaseG_chunk(m)

        tr.barrier()

        ZB2 = [view(RTMP, 8192 + i * 512, 512, BF16) for i in range(2)]
        ZQ2 = [view(RTMP, 8192 + 1024 + i * 512, 512, BF16) for i in range(2)]

        def ln_stats_chunk(mo, first, last, alt=False):
            j = mo % 2
            zb, zq = (ZB2[j], ZQ2[j]) if alt else (ZB[j], ZQ[j])
            tr.op("act", lambda e: e.activation(zb, Z[:, mo, :], AF.Copy), reads=(f"z{mo}",), writes=(f"zb{j}",))
            tr.op("act", lambda e: e.activation(zq, Z[:, mo, :], AF.Square), reads=(f"z{mo}",), writes=(f"zq{j}",))
            fns = []
            for h in range(2):
                fns.append(lambda e, h=h: e.matmul(ps[:, 4 + h, :], ones[:], zb[:, h * 512:(h + 1) * 512], start=first, stop=last))
                fns.append(lambda e, h=h: e.matmul(ps[:, 6 + h, :], ones[:], zq[:, h * 512:(h + 1) * 512], start=first, stop=last))
            wr = ()
            if first:
                wr = ("ps4", "ps5", "ps6", "ps7")
            if last:
                wr = ("st",)
            tr.group("pe", fns, reads=(f"zb{j}", f"zq{j}", "ones"), writes=wr, skip_self=True)

        def ln_finish(gname, bname, to_bf16, store):
            mean, var, nmr, msq = LNT
            s1 = ps[:, 4:6, :].rearrange("p a b -> p (a b)")
            s2 = ps[:, 6:8, :].rearrange("p a b -> p (a b)")
            tr.op("dve", lambda e: e.tensor_scalar_mul(mean, s1, 1.0 / D), reads=("st",), writes=("mean",))
            tr.op("dve", lambda e: e.tensor_mul(msq, mean, mean), reads=("mean",), writes=("msq",))
            tr.op("dve", lambda e: e.scalar_tensor_tensor(var, s2, 1.0 / D, msq, ALU.mult, ALU.subtract), reads=("st", "msq"), writes=("var",))
            tr.op("act", lambda e: e.activation(var, var, AF.Sqrt, bias=EPS, scale=1.0), reads=("var",), writes=("var",))
            tr.op("dve", lambda e: e.reciprocal(var, var), reads=("var",), writes=("var",))
            tr.op("dve", lambda e: e.scalar_tensor_tensor(nmr, mean, -1.0, var, ALU.mult, ALU.mult), reads=("mean", "var"), writes=("nmr",))
            for mo in range(16):
                zc = Z[:, mo, :]
                tr.op("dve", lambda e, zc=zc: e.tensor_mul(zc, zc, var), reads=(f"z{mo}", "var"), writes=(f"z{mo}",))
                tr.op("dve", lambda e, zc=zc: e.tensor_add(zc, zc, nmr), reads=(f"z{mo}", "nmr"), writes=(f"z{mo}",))
                tr.op("act", lambda e, zc=zc, mo=mo: e.activation(zc, zc, AF.Identity, bias=vcol(bname, mo), scale=vcol(gname, mo)),
                      reads=(f"z{mo}", "vec"), writes=(f"z{mo}",))
                if to_bf16:
                    tr.op("act", lambda e, zc=zc, mo=mo: e.activation(Y[:, mo, :], zc, AF.Copy), reads=(f"z{mo}",), writes=(f"y{mo}",))
                if store:
                    tr.dma("sp", lambda e, mo=mo: e.dma_start(out=out_v[:, mo, :], in_=Z[:, mo, :]), f"o{mo % 4}",
                           reads=(f"z{mo}",), writes=())

        out_v = out_d.rearrange("(k p) n -> p k n", p=128)
        xo_f = xo_d.rearrange("(k p) n -> p k n", p=128)

        load_xres(2)
        load_xres(3)
        def outproj_chunk(mo):
            s = need(B_OUT + mo // 2)
            cs = (mo % 2) * 128
            for h in range(2):
                bk = (2 * mo + h) % 4
                mm_group([(ps[:, bk, :], (h * 512, 512))],
                         lambda k: wbuf[:, s, k, cs:cs + 128],
                         lambda k, a: Y[:, k, a[0]:a[0] + a[1]], 16,
                         reads=(f"wb{s}",) + tuple(f"y{k}" for k in range(16)), writes=(f"ps{bk}",))
                zc = Z[:, mo, h * 512:(h + 1) * 512]
                tr.op("dve", lambda e, zc=zc, bk=bk: e.scalar_tensor_tensor(zc, zc, ALPHA, ps[:, bk, :], ALU.mult, ALU.add),
                      reads=(f"ps{bk}", f"z{mo}"), writes=(f"z{mo}",))
            if mo >= 1:
                ln_stats_chunk(mo - 1, mo == 1, False)
        for mo in range(16):
            outproj_chunk(mo)
        ln_stats_chunk(15, False, True)
        if debug == "z1":
            dump_and_finish(Z, 0, 16)
            return nc
        ln_finish("g1", "b1", True, False)
        if debug == "x2":
            dump_and_finish(Z, 0, 16)
            return nc

        ev = [0]

        def mlp_w1(q, fl):
            base = B_MLP + q * 16
            if True:
                s = need(base + fl // 2)
                cs = (fl % 2) * 128
                for h in range(2):
                    bk = ev[0] % 8
                    j = ev[0] % 2
                    ev[0] += 1
                    if q == 0 and fl == 0:
                        for kk in range(16):
                            mm_group([(ps[:, bk, :], (h * 512, 512))],
                                     lambda k: wbuf[:, s, k, cs:cs + 128],
                                     lambda k, a: Y[:, k, a[0]:a[0] + a[1]], 16,
                                     reads=(f"wb{s}", f"y{kk}"), writes=(f"ps{bk}",), k0=kk, k1=kk + 1, skip_self=(kk > 0))
                    else:
                        mm_group([(ps[:, bk, :], (h * 512, 512))],
                                 lambda k: wbuf[:, s, k, cs:cs + 128],
                                 lambda k, a: Y[:, k, a[0]:a[0] + a[1]], 16,
                                 reads=(f"wb{s}",) + tuple(f"y{k}" for k in range(16)), writes=(f"ps{bk}",))
                    tr.op("act", lambda e, bk=bk, j=j: e.activation(RL[j], ps[:, bk, :], AF.Relu), reads=(f"ps{bk}",), writes=(f"rl{j}",))
                    tr.op("dve", lambda e, fl=fl, h=h, j=j: e.tensor_mul(HQ[:, fl, h * 512:(h + 1) * 512], RL[j], RL[j]),
                          reads=(f"rl{j}",), writes=(f"hq{fl}",))
        def mlp_w2(q, mo):
            base = B_MLP + q * 16
            if True:
                s = need(base + 8 + mo // 2)
                cs = (mo % 2) * 128
                for h in range(2):
                    bk = ev[0] % (4 if q == 3 else 8)
                    ev[0] += 1
                    mm_group([(ps[:, bk, :], (h * 512, 512))],
                             lambda k: wbuf[:, s, k, cs:cs + 128],
                             lambda k, a: HQ[:, k, a[0]:a[0] + a[1]], 16,
                             reads=(f"wb{s}",) + tuple(f"hq{k}" for k in range(16)), writes=(f"ps{bk}",))
                    zc = Z[:, mo, h * 512:(h + 1) * 512]
                    if q == 0:
                        tr.op("dve", lambda e, zc=zc, bk=bk: e.scalar_tensor_tensor(zc, zc, ALPHA, ps[:, bk, :], ALU.mult, ALU.add),
                              reads=(f"ps{bk}", f"z{mo}"), writes=(f"z{mo}",))
                    else:
                        tr.op("dve", lambda e, zc=zc, bk=bk: e.tensor_add(zc, zc, ps[:, bk, :]),
                              reads=(f"ps{bk}", f"z{mo}"), writes=(f"z{mo}",))
        for q in range(4):
            for fl in range(16):
                mlp_w1(q, fl)
            for mo in range(16):
                mlp_w2(q, mo)
                if q == 3 and mo >= 1:
                    ln_stats_chunk(mo - 1, mo == 1, False, alt=True)

        ln_stats_chunk(15, False, True, alt=True)
        ln_finish("g2", "b2", False, True)
        tr.barrier()

        tr.emit()
    return nc


_PASSES = {0: [(3, 1), (2, 1), (1, 1), (0, 0), (0, 1)],
           1: [(0, 0), (3, 1), (2, 1), (1, 0), (1, 1)],
           2: [(0, 0), (1, 0), (3, 1), (2, 0), (2, 1)],
           3: [(0, 0), (1, 0), (2, 0), (3, 0), (3, 1)]}
_KEEP = {0: (1, 1), 1: (0, 1), 2: (1, 0), 3: (1, 1)}
_SELF = {0: (0, 0, 0), 1: (1, 0, 0), 2: (0, 1, 0), 3: (0, 0, 1)}
_SELB = {0: (0, 0, 1), 1: (0, 0, 1), 2: (0, 0, 1), 3: (0, 0, 0)}


def _chan(v):
    return np.ascontiguousarray(v.reshape(-1, 128).T)


def kernel(x, ln_mix_g, ln_mix_b, w_in, w_pool, pool_scale, conv_w, conv_b, w_rg_a, b_rg_a, w_rg_i, b_rg_i,
           rg_lambda, w_out, ln_ffn_g, ln_ffn_b, w_mlp_in, w_mlp_out):
    x = np.asarray(x, np.float32)
    f = lambda a: np.ascontiguousarray(np.asarray(a, np.float32))
    w_in0, w_out0, w10, w20, wp0 = f(w_in[0]), f(w_out[0]), f(w_mlp_in[0]), f(w_mlp_out[0]), f(w_pool[0])
    cw = np.asarray(conv_w[0], np.float32)[:, 0, :]
    cbv = np.asarray(conv_b[0], np.float32)
    wa, wi = np.asarray(w_rg_a[0], np.float32), np.asarray(w_rg_i[0], np.float32)
    ba, bi, lam = (np.asarray(a[0], np.float32) for a in (b_rg_a, b_rg_i, rg_lambda))
    in_maps = []
    for core in range(8):
        b, c = core // 4, core % 4
        xT = x[b].T
        xpad = np.zeros((D, S + 16), np.float32)
        xpad[:, 8:8 + S] = xT
        xs = np.zeros((4, D, TS), np.float32)
        wg = np.zeros((NPASS, 2, 8, 128, 128), np.float32)
        vec = np.zeros((128, NV), np.float32)
        for p, (ch, rev) in enumerate(_PASSES[c]):
            if not rev:
                t0 = ch * T - 1
                if p < 4:
                    xs[p, :, :1027] = xpad[:, 8 + t0:8 + t0 + 1027]
                taps = [cw[0], cw[1], cw[2], cw[3]]
            else:
                t0 = (ch + 1) * T + 1
                if p < 4:
                    xs[p, :, :1027] = xpad[:, 8 + t0 - 1026:8 + t0 + 1][:, ::-1]
                taps = [cw[3], cw[2], cw[1], cw[0]]
            wg[p, 0], wg[p, 1] = wa[rev], wi[rev]
            for k in range(4):
                vec[:, VEC[f"tap{p}_{k}"]:VEC[f"tap{p}_{k}"] + 8] = _chan(taps[k])
            vec[:, VEC[f"cb{p}"]:VEC[f"cb{p}"] + 8] = _chan(cbv)
            vec[:, VEC[f"ba{p}"]:VEC[f"ba{p}"] + 8] = _chan(ba[rev])
            vec[:, VEC[f"bi{p}"]:VEC[f"bi{p}"] + 8] = _chan(bi[rev])
            vec[:, VEC[f"lam{p}"]:VEC[f"lam{p}"] + 8] = _chan(lam[rev])
        vec[:, VEC["keep1"]], vec[:, VEC["keep2"]] = _KEEP[c]
        for j in range(3):
            vec[:, VEC[f"selF{j}"]] = _SELF[c][j]
            vec[:, VEC[f"selB{j}"]] = _SELB[c][j]
        vec[:, VEC["ps"]:VEC["ps"] + 8] = _chan(np.asarray(pool_scale[0], np.float32))
        vec[:, VEC["g1"]:VEC["g1"] + 16] = _chan(np.asarray(ln_mix_g[0], np.float32))
        vec[:, VEC["b1"]:VEC["b1"] + 16] = _chan(np.asarray(ln_mix_b[0], np.float32))
        vec[:, VEC["g2"]:VEC["g2"] + 16] = _chan(np.asarray(ln_ffn_g[0], np.float32))
        vec[:, VEC["b2"]:VEC["b2"] + 16] = _chan(np.asarray(ln_ffn_b[0], np.float32))
        for g, w in enumerate(WINDOWS):
            for e in range(16):
                tl = e if e < 8 else 1016 + (e - 8)
                t = c * T + tl
                cnt = min(t + w // 2, S) - max(t - w // 2, 0)
                vec[:, VEC["invc"] + g * 16 + e] = 1.0 / cnt
        xo = np.ascontiguousarray(xpad[:, c * T:c * T + TO])
        in_maps.append({"xs": xs, "xo": xo, "w_in": w_in0, "w_out": w_out0, "w1": w10, "w2": w20,
                        "wp": wp0, "wg": wg, "vec": vec})
    import os
    nc = build_nc(os.environ.get("KDEBUG") or None)
    res = run_bass_kernel_spmd(nc, in_maps, core_ids=list(range(8)))
    out = np.empty((2, S, D), np.float32)
    for core in range(8):
        b, c = core // 4, core % 4
        out[b, c * T:(c + 1) * T, :] = res.results[core]["out"].T
    return out
```

```python
import numpy as np
from contextlib import ExitStack
import concourse.bass as bass
import concourse.mybir as mybir
from concourse.bass_utils import run_bass_kernel_spmd

F32, BF16 = mybir.dt.float32, mybir.dt.bfloat16
AF = mybir.ActivationFunctionType
ALU = mybir.AluOpType

D = 2048
S = 4096
T = 1024
NPASS = 5
TS = 1028
TO = 1040
DFF = 8192
ALPHA = 2.0 ** 0.25
EPS = 1e-5
WINDOWS = (2, 4, 8, 16)
WBLK = 256
NSLOT = 4

VEC = {}
_off = 0
def _v(name, n):
    global _off
    VEC[name] = _off
    _off += n
for _p in range(NPASS):
    for _k in range(4):
        _v(f"tap{_p}_{_k}", 8)
    _v(f"cb{_p}", 8); _v(f"ba{_p}", 8); _v(f"bi{_p}", 8); _v(f"lam{_p}", 8)
_v("keep1", 1); _v("keep2", 1)
for _j in range(3):
    _v(f"selF{_j}", 1)
for _j in range(3):
    _v(f"selB{_j}", 1)
_v("ps", 8); _v("g1", 16); _v("b1", 16); _v("g2", 16); _v("b2", 16)
_v("invc", 64)
NV = _off


class Tr:
    ENG = ["pe", "act", "dve", "pool", "sp"]

    def __init__(self, nc, es):
        self.nc, self.es = nc, es
        self.semh = {"E" + e: es.enter_context(nc.semaphore("s_" + e)) for e in self.ENG}
        self.cnt = {e: 0 for e in self.ENG}
        self.stream = {e: [] for e in self.ENG}
        self.res = {}
        self.waited = {e: {} for e in self.ENG}
        self.dcnt = {}

    def _deps(self, eng, reads, writes, skip_self=False):
        deps = {}

        def add(s, v):
            if deps.get(s, 0) < v:
                deps[s] = v
        for k in reads:
            r = self.res.get(k)
            if r and r["w"]:
                add(*r["w"])
        for k in writes:
            r = self.res.get(k)
            if r:
                if r["w"]:
                    add(*r["w"])
                for s, v in r["r"].items():
                    add(s, v)
        out = []
        for s, v in deps.items():
            if skip_self and s == "E" + eng:
                continue
            if self.waited[eng].get(s, 0) >= v:
                continue
            self.waited[eng][s] = v
            out.append((s, v))
        return out

    def _mark(self, reads, writes, sv):
        for k in reads:
            r = self.res.setdefault(k, {"w": None, "r": {}})
            if r["r"].get(sv[0], 0) < sv[1]:
                r["r"][sv[0]] = sv[1]
        for k in writes:
            self.res[k] = {"w": sv, "r": {}}

    def group(self, eng, fns, reads=(), writes=(), skip_self=False):
        deps = self._deps(eng, reads, writes, skip_self)
        self.cnt[eng] += 1
        sv = ("E" + eng, self.cnt[eng])
        self.stream[eng].append((deps, list(fns), (sv[0], 1)))
        self._mark(reads, writes, sv)

    def op(self, eng, fn, reads=(), writes=()):
        self.group(eng, [fn], reads, writes)

    def dma(self, eng, fn, sem, reads=(), writes=()):
        key = "D" + sem
        if key not in self.semh:
            self.semh[key] = self.es.enter_context(self.nc.semaphore("d_" + sem))
            self.dcnt[key] = 0
        deps = self._deps(eng, reads, writes)
        self.dcnt[key] += 16
        sv = (key, self.dcnt[key])
        self.stream[eng].append((deps, [fn], (key, 16)))
        self._mark(reads, writes, sv)

    def barrier(self):
        for e in self.ENG:
            deps = []
            allv = [("E" + e2, self.cnt[e2]) for e2 in self.ENG] + list(self.dcnt.items())
            for s, v in allv:
                if v > 0 and self.waited[e].get(s, 0) < v:
                    self.waited[e][s] = v
                    deps.append((s, v))
            self.stream[e].append((deps, [], None))
        self.res = {}

    def wait_all(self, e):
        deps = []
        allv = [("E" + e2, self.cnt[e2]) for e2 in self.ENG] + list(self.dcnt.items())
        for s, v in allv:
            if v > 0 and self.waited[e].get(s, 0) < v:
                self.waited[e][s] = v
                deps.append((s, v))
        self.stream[e].append((deps, [], None))

    def emit(self):
        nc = self.nc
        with nc.Block() as block:
            def mk(e):
                def run(eng):
                    for deps, fns, inc in self.stream[e]:
                        for s, v in deps:
                            eng.wait_ge(self.semh[s], v)
                        ins = None
                        for f in fns:
                            ins = f(eng)
                        if inc is not None and ins is not None:
                            ins.then_inc(self.semh[inc[0]], inc[1])
                return run
            block.tensor(mk("pe"))
            block.scalar(mk("act"))
            block.vector(mk("dve"))
            block.gpsimd(mk("pool"))
            block.sync(mk("sp"))


def build_nc(debug=None):
    nc = bass.Bass("TRN2", target_bir_lowering=False)
    xs_d = nc.dram_tensor("xs", [4, D, TS], F32, kind="ExternalInput").ap()
    xo_d = nc.dram_tensor("xo", [D, TO], F32, kind="ExternalInput").ap()
    win_d = nc.dram_tensor("w_in", [D, 3072], F32, kind="ExternalInput").ap()
    wout_d = nc.dram_tensor("w_out", [D, D], F32, kind="ExternalInput").ap()
    w1_d = nc.dram_tensor("w1", [D, DFF], F32, kind="ExternalInput").ap()
    w2_d = nc.dram_tensor("w2", [DFF, D], F32, kind="ExternalInput").ap()
    wp_d = nc.dram_tensor("wp", [4, 256, 256], F32, kind="ExternalInput").ap()
    wg_d = nc.dram_tensor("wg", [NPASS, 2, 8, 128, 128], F32, kind="ExternalInput").ap()
    vec_d = nc.dram_tensor("vec", [128, NV], F32, kind="ExternalInput").ap()
    out_d = nc.dram_tensor("out", [D, T], F32, kind="ExternalOutput").ap()

    es = ExitStack()
    with es:
        def sb(name, shape, dt):
            return es.enter_context(nc.sbuf_tensor(name, shape, dt))
        XW = 16 * TO // 2
        RX = sb("rx", [128, 2 * XW], F32)
        RHS = sb("rhs", [128, 8192], F32)
        RTMP = sb("rtmp", [128, 14848], F32)
        wbuf = sb("wbuf", [128, NSLOT, 16, WBLK], BF16)
        gw = sb("gw", [128, 3, 2, 8, 128], BF16)
        wp = sb("wpool", [128, 4, 2, 256], BF16)
        vec = sb("vec_sb", [128, NV], F32)
        dv = sb("dv", [128, NPASS, 4, 8], F32)
        sc = sb("sc", [128, 8, 40], F32)
        est = sb("est", [128, 3, 8], F32)
        ini = sb("ini", [128, 4, 8], F32)
        ones = sb("ones", [128, 128], BF16)
        ps = es.enter_context(nc.psum_tensor("psum_all", [128, 8, 512], F32))

        def view(raw, off, n, dt, pat=None, **kw):
            v = raw[:, off:off + n]
            if dt != F32:
                v = v.bitcast(dt)
            if pat:
                v = v.rearrange(pat, **kw)
            return v
        XT = [view(RX, i * XW, XW, BF16, "p (k n) -> p k n", k=16) for i in range(2)]
        Z = view(RX, 0, 16384, F32, "p (k n) -> p k n", k=16)
        HS = view(RHS, 0, 8192, F32, "p (k n) -> p k n", k=8)
        HQ = view(RHS, 0, 8192, BF16, "p (k n) -> p k n", k=16)
        LNT = [view(RHS, i * 1024, 1024, F32) for i in range(4)]
        ZB = [view(RHS, 4096 + i * 512, 512, BF16) for i in range(2)]
        ZQ = [view(RHS, 5120 + i * 512, 512, BF16) for i in range(2)]
        o = 0
        XC = [view(RTMP, o + i * 1024, 1024, F32) for i in range(3)]; o += 3072
        XCB = [view(RTMP, o + i * 512, 512, BF16) for i in range(3)]; o += 1536
        TR = view(RTMP, o, 1024, F32); o += 1024
        TI = [view(RTMP, o + i * 1024, 1024, F32) for i in range(3)]; o += 3072
        AA = [view(RTMP, o + i * 1024, 1024, F32) for i in range(3)]; o += 3072
        A2 = [view(RTMP, o + i * 1024, 1024, F32) for i in range(3)]; o += 3072
        assert o <= 14848
        Y = view(RTMP, 0, 8192, BF16, "p (k n) -> p k n", k=16)
        GL = [view(RTMP, 8192 + i * 1024, 1024, F32) for i in range(2)]
        RL = [view(RTMP, 8192 + 2048 + i * 512, 512, F32) for i in range(2)]
        UP = [view(RX, i * 1040, 1040, F32) for i in range(2)]
        PT = [view(RX, 2080 + i * 1040, 1040, F32) for i in range(2)]
        DP = view(RX, 4160, 4096, BF16, "p (k n) -> p k n", k=8)
        T8 = view(RX, 8256, 8, F32)
        assert 8264 <= XW

        tr = Tr(nc, es)

        def vcol(name, i=0, n=1):
            return vec[:, VEC[name] + i: VEC[name] + i + n]

        blocks = []
        win_v = win_d.rearrange("(k p) n -> p k n", p=128)
        wout_v = wout_d.rearrange("(k p) n -> p k n", p=128)
        w1_v = w1_d.rearrange("(k p) n -> p k n", p=128)
        w2_v = w2_d.rearrange("(q k p) n -> q p k n", p=128, k=16)
        for p in range(4):
            for j in range(4):
                blocks.append(win_v[:, :, 1024 + j * WBLK:1024 + (j + 1) * WBLK])
        B_POOL = len(blocks)
        for j in range(4):
            blocks.append(win_v[:, :, j * WBLK:(j + 1) * WBLK])
        B_GATE = len(blocks)
        for j in range(4):
            blocks.append(win_v[:, :, 2048 + j * WBLK:2048 + (j + 1) * WBLK])
        B_OUT = len(blocks)
        for j in range(8):
            blocks.append(wout_v[:, :, j * WBLK:(j + 1) * WBLK])
        B_MLP = len(blocks)
        for q in range(4):
            for j in range(8):
                blocks.append(w1_v[:, :, q * 2048 + j * WBLK:q * 2048 + (j + 1) * WBLK])
            for j in range(8):
                blocks.append(w2_v[q][:, :, j * WBLK:(j + 1) * WBLK])
        issued = [0]

        def need(bi, look=3):
            while issued[0] < len(blocks) and issued[0] <= bi + look:
                b = issued[0]
                s = b % NSLOT
                src = blocks[b]
                tr.dma("pool", lambda e, s=s, src=src: e.dma_start(out=wbuf[:, s], in_=src),
                       f"w{s}", reads=(), writes=(f"wb{s}",))
                issued[0] += 1
            return bi % NSLOT

        def mm_group(bank_cols, lhs_fn, rhs_fn, nk, reads, writes, k0=0, k1=None, skip_self=False):
            fns = []
            for k in range(k0, nk if k1 is None else k1):
                for (pap, rarg) in bank_cols:
                    l, r = lhs_fn(k), rhs_fn(k, rarg)
                    fns.append(lambda e, pap=pap, l=l, r=r, k=k: e.matmul(
                        pap, l, r, start=(k == 0), stop=(k == nk - 1)))
            tr.group("pe", fns, reads, writes, skip_self=skip_self)

        tr.dma("sp", lambda e: e.dma_start(out=vec[:], in_=vec_d), "vec", writes=("vec",))
        tr.op("dve", lambda e: e.memset(ones[:], 1.0), writes=("ones",))
        tr.dma("pool", lambda e: e.dma_start(out=wp[:], in_=wp_d.rearrange("g (c i) o -> i g c o", i=128)),
               "wp", writes=("wp",))
        lamv = vec[:, VEC["lam0"]:VEC["lam0"] + 8]
        for p in range(NPASS):
            lam = vec[:, VEC[f"lam{p}"]:VEC[f"lam{p}"] + 8]
            ee, zz, z2, acc, tmp = (sc[:, i, p * 8:(p + 1) * 8] for i in range(5))
            tr.op("act", lambda e, ee=ee, lam=lam: e.activation(ee, lam, AF.Exp, scale=-1.0),
                  reads=("vec",), writes=(f"sc{p}",))
            tr.op("dve", lambda e, ee=ee, tmp=tmp: e.tensor_scalar_add(tmp, ee, 2.0), reads=(f"sc{p}",), writes=(f"sc{p}",))
            tr.op("dve", lambda e, tmp=tmp: e.reciprocal(tmp, tmp), reads=(f"sc{p}",), writes=(f"sc{p}",))
            tr.op("dve", lambda e, zz=zz, ee=ee, tmp=tmp: e.tensor_mul(zz, ee, tmp), reads=(f"sc{p}",), writes=(f"sc{p}",))
            tr.op("dve", lambda e, zz=zz, z2=z2: e.tensor_mul(z2, zz, zz), reads=(f"sc{p}",), writes=(f"sc{p}",))
            tr.op("dve", lambda e, acc=acc, z2=z2: e.tensor_scalar(acc, z2, 1.0 / 11, 1.0 / 9, ALU.mult, ALU.add),
                  reads=(f"sc{p}",), writes=(f"sc{p}",))
            for cst in (1.0 / 7, 1.0 / 5, 1.0 / 3, 1.0):
                tr.op("dve", lambda e, acc=acc, z2=z2: e.tensor_mul(acc, acc, z2), reads=(f"sc{p}",), writes=(f"sc{p}",))
                tr.op("dve", lambda e, acc=acc, cst=cst: e.tensor_scalar_add(acc, acc, cst), reads=(f"sc{p}",), writes=(f"sc{p}",))
            tr.op("dve", lambda e, acc=acc, zz=zz: e.tensor_mul(acc, acc, zz), reads=(f"sc{p}",), writes=(f"sc{p}",))
            tr.op("dve", lambda e, acc=acc, p=p: e.tensor_scalar_mul(dv[:, p, 0, :], acc, -8.0), reads=(f"sc{p}",), writes=(f"dv{p}",))
            tr.op("dve", lambda e, acc=acc, p=p: e.tensor_scalar_mul(dv[:, p, 1, :], acc, -16.0), reads=(f"sc{p}",), writes=(f"dv{p}",))
            tr.op("dve", lambda e, p=p: e.tensor_scalar_mul(dv[:, p, 2, :], vec[:, VEC[f"ba{p}"]:VEC[f"ba{p}"] + 8], 0.5),
                  reads=("vec",), writes=(f"dv{p}",))
            tr.op("dve", lambda e, p=p: e.tensor_scalar_mul(dv[:, p, 3, :], vec[:, VEC[f"bi{p}"]:VEC[f"bi{p}"] + 8], 0.5),
                  reads=("vec",), writes=(f"dv{p}",))

        xs_v = xs_d.rearrange("s (k p) n -> s p k n", p=128)
        xo_v = xo_d.rearrange("(k p) n -> p k n", p=128)
        wg_v = wg_d.rearrange("s g h i o -> s i g h o")

        GWI = {0: 0, 1: 1, 2: 0, 3: 1, 4: 2}

        def load_pass_inputs(p):
            if p < 4:
                xt = XT[p % 2]
                tr.dma("pool", lambda e, p=p, xt=xt: e.dma_start(out=xt[:, :, 0:TS], in_=xs_v[p]),
                       f"x{p % 2}", writes=(f"xt{p % 2}",))
            tr.dma("pool", lambda e, p=p: e.dma_start(out=gw[:, GWI[p]], in_=wg_v[p]),
                   f"gw{GWI[p]}", writes=(f"gw{GWI[p]}",))

        need(0, look=0)
        load_pass_inputs(0)
        tr.op("pool", lambda e: e.memset(sc[:, 7, 0:8], 0.0), reads=("xt0", "wb0", "gw0"), writes=("scpad",))
        need(0, look=1)
        load_pass_inputs(1)
        load_pass_inputs(4)
        units = [(p, m) for p in range(3) for m in range(8)]
        for m in range(8):
            units += [(3, m), (4, m)]
        FULL, XI, FC = [], [], []
        fc = 0
        for (p, m) in units:
            if p < 4:
                FULL.append(True); XI.append(fc % 3); FC.append(fc); fc += 1
            else:
                FULL.append(False); XI.append(XI[-1]); FC.append(FC[-1])
        NU = len(units)
        TT = [(0, 512), (512, 512), (1024, 4)]

        def e_inproj(ci, part):
            p, m = units[ci]
            xt = XT[p % 2]
            bi = p * 4 + m // 2
            s = need(bi) if part == 0 else bi % NSLOT
            cs = (m % 2) * 128
            ub = 3 * (FC[ci] % 2)
            ukeys = tuple(f"ps{ub + tt}" for tt in range(3))
            mm_group([(ps[:, ub + tt, 0:n], (c0, n)) for tt, (c0, n) in enumerate(TT)],
                     lambda k: wbuf[:, s, k, cs:cs + 128],
                     lambda k, a: xt[:, k, a[0]:a[0] + a[1]], 16,
                     reads=(f"wb{s}", f"xt{p % 2}"), writes=ukeys,
                     k0=0 if part == 0 else 3, k1=3 if part == 0 else 16, skip_self=(part == 1))
            if part == 1:
                if m == 1 and p >= 1 and p + 1 < 4:
                    load_pass_inputs(p + 1)
                if m == 7 and p == 3:
                    tr.dma("pool", lambda e: e.dma_start(out=XT[1], in_=xo_v), "x1", writes=("xt1",))

        def conv_args(ci):
            p, m = units[ci]
            ub = 3 * (FC[ci] % 2)
            ukeys = tuple(f"ps{ub + tt}" for tt in range(3))
            tap = lambda k: vec[:, VEC[f"tap{p}_{k}"] + m:VEC[f"tap{p}_{k}"] + m + 1]
            cb = vec[:, VEC[f"cb{p}"] + m:VEC[f"cb{p}"] + m + 1]
            return ub, ukeys, tap, cb

        def e_tap0(ci):
            ub, ukeys, tap, cb = conv_args(ci)
            x3 = XC[XI[ci]].rearrange("p (a b) -> p a b", a=2)
            tr.op("act", lambda e: e.activation(x3, ps[:, ub:ub + 2, :], AF.Identity, bias=cb, scale=tap(0)),
                  reads=ukeys[:2] + ("vec",), writes=(f"xc{XI[ci]}",))

        def e_taps(ci):
            ub, ukeys, tap, cb = conv_args(ci)
            x = XC[XI[ci]]
            uflat = ps[:, ub:ub + 3, :].rearrange("p a b -> p (a b)")
            for k in (1, 2, 3):
                tr.op("dve", lambda e, k=k: e.scalar_tensor_tensor(x, uflat[:, k:k + 1024], tap(k), x, ALU.mult, ALU.add),
                      reads=ukeys + (f"xc{XI[ci]}", "vec"), writes=(f"xc{XI[ci]}",))

        def e_cast(ci):
            tr.op("act", lambda e: e.activation(XCB[XI[ci]], XC[XI[ci]], AF.Copy), reads=(f"xc{XI[ci]}",), writes=(f"xcb{XI[ci]}",))

        def e_gate(ci, g):
            p, m = units[ci]
            j = ci % 3
            xi = XI[ci]
            fns = [lambda e, h=h: e.matmul(ps[:, 6 + h, :], gw[:, GWI[p], g, m, :],
                                           XCB[xi][:, h * 512:(h + 1) * 512], start=True, stop=True) for h in range(2)]
            tr.group("pe", fns, reads=(f"xcb{xi}", f"gw{GWI[p]}"), writes=("ps6", "ps7"))

        def e_tanh(ci, g):
            p, m = units[ci]
            j = ci % 3
            dstt, dkey = (TR, "tr") if g == 0 else (TI[j], f"ti{j}")
            hb = dv[:, p, 2 + g, m:m + 1]
            d3 = dstt.rearrange("p (a b) -> p a b", a=2)
            tr.op("act", lambda e: e.activation(d3, ps[:, 6:8, :], AF.Tanh, bias=hb, scale=0.5),
                  reads=("ps6", "ps7", f"dv{p}"), writes=(dkey,))

        def e_exps(ci):
            p, m = units[ci]
            j = ci % 3
            hc, cc = dv[:, p, 0, m:m + 1], dv[:, p, 1, m:m + 1]
            tr.op("act", lambda e: e.activation(AA[j], TR, AF.Exp, bias=hc, scale=hc), reads=("tr", f"dv{p}"), writes=(f"aa{j}",))
            tr.op("act", lambda e: e.activation(A2[j], TR, AF.Exp, bias=cc, scale=cc), reads=("tr", f"dv{p}"), writes=(f"a2{j}",))

        def e_D2(ci):
            j = ci % 3
            xi = XI[ci]
            tr.op("dve", lambda e: e.scalar_tensor_tensor(TI[j], TI[j], 1.0, XC[xi], ALU.add, ALU.mult),
                  reads=(f"ti{j}", f"xc{xi}"), writes=(f"ti{j}",))

        def stageA7(ci):
            j = ci % 3
            tr.op("act", lambda e: e.activation(A2[j], A2[j], AF.Sqrt, bias=1.0, scale=-1.0), reads=(f"a2{j}",), writes=(f"a2{j}",))

        def stageB2(ci):
            p, m = units[ci]
            j = ci % 3
            tr.op("dve", lambda e: e.scalar_tensor_tensor(TI[j], TI[j], 0.5, A2[j], ALU.mult, ALU.mult),
                  reads=(f"ti{j}", f"a2{j}"), writes=(f"ti{j}",))
            if p == 0:
                init, ird = 0.0, ()
            else:
                init, ird = ini[:, p - 1, m:m + 1], ("ini",)
            if p == 4:
                tr.op("dve", lambda e: e.tensor_tensor_scan(TI[j][:, ::-1], AA[j][:, ::-1], TI[j][:, ::-1], init, ALU.mult, ALU.add),
                      reads=(f"aa{j}", f"ti{j}") + ird, writes=(f"ti{j}",))
                tr.op("dve", lambda e: e.tensor_add(HS[:, m, :], HS[:, m, :], TI[j]), reads=(f"ti{j}", f"hs{m}"), writes=(f"hs{m}",))
            elif p == 3:
                tr.op("dve", lambda e: e.tensor_tensor_scan(HS[:, m, :], AA[j], TI[j], init, ALU.mult, ALU.add),
                      reads=(f"aa{j}", f"ti{j}") + ird, writes=(f"hs{m}",))
            else:
                tr.op("dve", lambda e: e.tensor_tensor_scan(TI[j], AA[j], TI[j], init, ALU.mult, ALU.add),
                      reads=(f"aa{j}", f"ti{j}") + ird, writes=(f"ti{j}",))
                tr.op("dve", lambda e: e.tensor_copy(est[:, p, m:m + 1], TI[j][:, 1023:1024]), reads=(f"ti{j}",), writes=("est",))
            if m == 7 and p < 2:
                kp = vcol(f"keep{p + 1}")
                tr.op("dve", lambda e: e.tensor_scalar_mul(ini[:, p, :], est[:, p, :], kp), reads=("est", "vec"), writes=("ini",))
            if m == 7 and p == 2:
                for q, nm in ((2, "selF"), (3, "selB")):
                    tr.op("dve", lambda e, q=q, nm=nm: e.tensor_scalar_mul(ini[:, q, :], est[:, 0, :], vcol(nm + "0")),
                          reads=("est", "vec"), writes=("ini",))
                    for jj in (1, 2):
                        tr.op("dve", lambda e, q=q, nm=nm, jj=jj: e.scalar_tensor_tensor(
                            ini[:, q, :], est[:, jj, :], vcol(nm + str(jj)), ini[:, q, :], ALU.mult, ALU.add),
                            reads=("est", "vec", "ini"), writes=("ini",))

        for t in range(NU + 5):
            u2, u1, uf = t - 2, t - 1, t - 5
            v2, v1 = 0 <= u2 < NU, 0 <= u1 < NU
            if 0 <= uf < NU:
                stageB2(uf)
            if v2:
                e_gate(u2, 0)
            if t < NU and FULL[t]:
                e_inproj(t, 0)
            if v2:
                e_tanh(u2, 0)
            if v1 and FULL[u1]:
                e_tap0(u1)
                e_taps(u1)
            if v2:
                e_gate(u2, 1)
            if t < NU and FULL[t]:
                e_inproj(t, 1)
            if v2:
                e_tanh(u2, 1)
                e_exps(u2)
                e_D2(u2)
            if v1 and FULL[u1]:
                e_cast(u1)
            if v2 and u2 % 2 == 1:
                stageA7(u2 - 1); stageA7(u2)

        if debug == "hs":
            tr.barrier()
        out_v = out_d.rearrange("(k p) n -> p k n", p=128)

        def dump_and_finish(src_ap, k0, nk):
            tr.barrier()
            tr.dma("sp", lambda e: e.dma_start(out=out_v[:, k0:k0 + nk, :], in_=src_ap), "o0", reads=(), writes=())
            tr.barrier()
            tr.emit()
        if debug == "hs":
            dump_and_finish(HS, 0, 8)
            return nc

        xo = XT[1]
        TTO = [(0, 512), (512, 512), (1024, 16)]
        def phaseP_chunk(m):
            g = m // 2
            w = WINDOWS[g]
            s = need(B_POOL + m // 2)
            cs = (m % 2) * 128
            bb = 3 * (m % 2)
            mm_group([(ps[:, bb + tt, 0:n], (c0, n)) for tt, (c0, n) in enumerate(TTO)],
                     lambda k: wbuf[:, s, k, cs:cs + 128],
                     lambda k, a: xo[:, k, a[0]:a[0] + a[1]], 16,
                     reads=(f"wb{s}", "xt1"), writes=tuple(f"ps{bb + tt}" for tt in range(3)))
            up = UP[m % 2]
            for tt, (c0, n) in enumerate(TTO):
                tr.op("act", lambda e, tt=tt, c0=c0, n=n, up=up, bb=bb: e.activation(up[:, c0:c0 + n], ps[:, bb + tt, 0:n], AF.Copy),
                      reads=(f"ps{bb + tt}",), writes=(f"up{m % 2}",))
            src, L, step, pi = up, 1040, 1, 0
            while step < w:
                dst = PT[pi]
                n = L - step
                tr.op("dve", lambda e, dst=dst, src=src, n=n, step=step: e.tensor_add(dst[:, 0:n], src[:, 0:n], src[:, step:step + n]),
                      reads=(f"up{m % 2}", "pt0", "pt1"), writes=(f"pt{pi}",))
                src, L, step, pi = dst, n, step * 2, 1 - pi
            st0 = 8 - w // 2
            W = src
            tr.op("dve", lambda e, W=W, up=up: e.scalar_tensor_tensor(DP[:, m, :], W[:, st0:st0 + 1024], 1.0 / w, up[:, 8:1032],
                                                                      ALU.mult, ALU.subtract),
                  reads=("pt0", "pt1", f"up{m % 2}"), writes=(f"dp{m}",))
            for (c0, e0) in ((0, 0), (1016, 8)):
                iv = vec[:, VEC["invc"] + g * 16 + e0:VEC["invc"] + g * 16 + e0 + 8]
                tr.op("dve", lambda e, W=W, c0=c0, iv=iv: e.tensor_mul(T8, W[:, st0 + c0:st0 + c0 + 8], iv),
                      reads=("pt0", "pt1", "vec"), writes=("t8",))
                tr.op("dve", lambda e, c0=c0, up=up: e.tensor_sub(DP[:, m, c0:c0 + 8], T8, up[:, 8 + c0:16 + c0]),
                      reads=("t8", f"up{m % 2}", f"dp{m}"), writes=(f"dp{m}",))
            if m % 2 == 1:
                if m == 1:
                    tr.wait_all("act")
                for oc in range(2):
                    for h in range(2):
                        bk = 6 + h
                        fns = [lambda e, ic=ic, oc=oc, h=h, bk=bk: e.matmul(ps[:, bk, :], wp[:, g, ic, oc * 128:(oc + 1) * 128],
                                                                           DP[:, 2 * g + ic, h * 512:(h + 1) * 512],
                                                                           start=(ic == 0), stop=(ic == 1)) for ic in range(2)]
                        tr.group("pe", fns, reads=("wp", f"dp{2 * g}", f"dp{2 * g + 1}"), writes=(f"ps{bk}",))
                        mo = 2 * g + oc
                        tr.op("act", lambda e, mo=mo, h=h, bk=bk: e.activation(Y[:, mo, h * 512:(h + 1) * 512], ps[:, bk, :], AF.Identity,
                                                                                scale=vcol("ps", mo)),
                              reads=(f"ps{bk}", "vec"), writes=(f"y{mo}",))

        for m in range(8):
            phaseP_chunk(m)
        out_v = out_d.rearrange("(k p) n -> p k n", p=128)
        xo_f = xo_d.rearrange("(k p) n -> p k n", p=128)

        def load_xres(q4):
            tr.dma("sp", lambda e: e.dma_start(out=Z[:, 4 * q4:4 * q4 + 4, :], in_=xo_f[:, 4 * q4:4 * q4 + 4, 8:1032]),
                   f"xr{q4}", writes=tuple(f"z{4 * q4 + i}" for i in range(4)) + (("xt1",) if q4 >= 2 else ()))
        tr.wait_all("sp")
        load_xres(0)
        load_xres(1)

        def phaseG_chunk(m):
            s = need(B_GATE + m // 2)
            cs = (m % 2) * 128
            bb = 2 * (m % 2)
            mm_group([(ps[:, bb + h, :], (8 + h * 512, 512)) for h in range(2)],
                     lambda k: wbuf[:, s, k, cs:cs + 128],
                     lambda k, a: xo[:, k, a[0]:a[0] + a[1]], 16,
                     reads=(f"wb{s}", "xt1"), writes=(f"ps{bb}", f"ps{bb + 1}"))
            gl = GL[m % 2]
            for h in range(2):
                tr.op("act", lambda e, h=h, gl=gl, bb=bb: e.activation(gl[:, h * 512:(h + 1) * 512], ps[:, bb + h, :], AF.Gelu_apprx_tanh),
                      reads=(f"ps{bb + h}",), writes=(f"gl{m % 2}",))
            tr.op("dve", lambda e, gl=gl: e.tensor_mul(Y[:, 8 + m, :], HS[:, m, :], gl),
                  reads=(f"gl{m % 2}", f"hs{m}"), writes=(f"y{8 + m}",))
        tr.wait_all("dve")
        for m in range(8):
            phaseG_chunk(m)


        ZB2 = [view(RTMP, 8192 + i * 512, 512, BF16) for i in range(2)]
        ZQ2 = [view(RTMP, 8192 + 1024 + i * 512, 512, BF16) for i in range(2)]

        def ln_stats_chunk(mo, first, last, alt=False):
            j = mo % 2
            zb, zq = (ZB2[j], ZQ2[j]) if alt else (ZB[j], ZQ[j])
            tr.op("act", lambda e: e.activation(zb, Z[:, mo, :], AF.Copy), reads=(f"z{mo}",), writes=(f"zb{j}",))
            tr.op("act", lambda e: e.activation(zq, Z[:, mo, :], AF.Square), reads=(f"z{mo}",), writes=(f"zq{j}",))
            fns = []
            for h in range(2):
                fns.append(lambda e, h=h: e.matmul(ps[:, 4 + h, :], ones[:], zb[:, h * 512:(h + 1) * 512], start=first, stop=last))
                fns.append(lambda e, h=h: e.matmul(ps[:, 6 + h, :], ones[:], zq[:, h * 512:(h + 1) * 512], start=first, stop=last))
            wr = ()
            if first:
                wr = ("ps4", "ps5", "ps6", "ps7")
            if last:
                wr = ("st",)
            tr.group("pe", fns, reads=(f"zb{j}", f"zq{j}", "ones"), writes=wr, skip_self=True)

        def ln_finish(gname, bname, to_bf16, store):
            mean, var, nmr, msq = LNT
            s1 = ps[:, 4:6, :].rearrange("p a b -> p (a b)")
            s2 = ps[:, 6:8, :].rearrange("p a b -> p (a b)")
            tr.op("dve", lambda e: e.tensor_scalar_mul(mean, s1, 1.0 / D), reads=("st",), writes=("mean",))
            tr.op("dve", lambda e: e.tensor_mul(msq, mean, mean), reads=("mean",), writes=("msq",))
            tr.op("dve", lambda e: e.scalar_tensor_tensor(var, s2, 1.0 / D, msq, ALU.mult, ALU.subtract), reads=("st", "msq"), writes=("var",))
            tr.op("act", lambda e: e.activation(var, var, AF.Sqrt, bias=EPS, scale=1.0), reads=("var",), writes=("var",))
            tr.op("dve", lambda e: e.reciprocal(var, var), reads=("var",), writes=("var",))
            tr.op("dve", lambda e: e.scalar_tensor_tensor(nmr, mean, -1.0, var, ALU.mult, ALU.mult), reads=("mean", "var"), writes=("nmr",))
            for mo in range(16):
                zc = Z[:, mo, :]
                tr.op("dve", lambda e, zc=zc: e.tensor_mul(zc, zc, var), reads=(f"z{mo}", "var"), writes=(f"z{mo}",))
                tr.op("dve", lambda e, zc=zc: e.tensor_add(zc, zc, nmr), reads=(f"z{mo}", "nmr"), writes=(f"z{mo}",))
                tr.op("act", lambda e, zc=zc, mo=mo: e.activation(zc, zc, AF.Identity, bias=vcol(bname, mo), scale=vcol(gname, mo)),
                      reads=(f"z{mo}", "vec"), writes=(f"z{mo}",))
                if to_bf16:
                    tr.op("act", lambda e, zc=zc, mo=mo: e.activation(Y[:, mo, :], zc, AF.Copy), reads=(f"z{mo}",), writes=(f"y{mo}",))
                if store:
                    tr.dma("sp", lambda e, mo=mo: e.dma_start(out=out_v[:, mo, :], in_=Z[:, mo, :]), f"o{mo % 4}",
                           reads=(f"z{mo}",), writes=())

        out_v = out_d.rearrange("(k p) n -> p k n", p=128)
        xo_f = xo_d.rearrange("(k p) n -> p k n", p=128)

        load_xres(2)
        load_xres(3)
        def outproj_chunk(mo):
            s = need(B_OUT + mo // 2)
            cs = (mo % 2) * 128
            for h in range(2):
                bk = (2 * mo + h) % 4
                mm_group([(ps[:, bk, :], (h * 512, 512))],
                         lambda k: wbuf[:, s, k, cs:cs + 128],
                         lambda k, a: Y[:, k, a[0]:a[0] + a[1]], 16,
                         reads=(f"wb{s}",) + tuple(f"y{k}" for k in range(16)), writes=(f"ps{bk}",))
                zc = Z[:, mo, h * 512:(h + 1) * 512]
                tr.op("dve", lambda e, zc=zc, bk=bk: e.scalar_tensor_tensor(zc, zc, ALPHA, ps[:, bk, :], ALU.mult, ALU.add),
                      reads=(f"ps{bk}", f"z{mo}"), writes=(f"z{mo}",))
            if mo >= 1:
                ln_stats_chunk(mo - 1, mo == 1, False)
        for mo in range(16):
            outproj_chunk(mo)
        ln_stats_chunk(15, False, True)
        if debug == "z1":
            dump_and_finish(Z, 0, 16)
            return nc
        ln_finish("g1", "b1", True, False)
        if debug == "x2":
            dump_and_finish(Z, 0, 16)
            return nc

        ev = [0]

        def mlp_w1(q, fl):
            base = B_MLP + q * 16
            if True:
                s = need(base + fl // 2)
                cs = (fl % 2) * 128
                for h in range(2):
                    bk = ev[0] % 8
                    j = ev[0] % 2
                    ev[0] += 1
                    if q == 0 and fl == 0:
                        for kk in range(16):
                            mm_group([(ps[:, bk, :], (h * 512, 512))],
                                     lambda k: wbuf[:, s, k, cs:cs + 128],
                                     lambda k, a: Y[:, k, a[0]:a[0] + a[1]], 16,
                                     reads=(f"wb{s}", f"y{kk}"), writes=(f"ps{bk}",), k0=kk, k1=kk + 1, skip_self=(kk > 0))
                    else:
                        mm_group([(ps[:, bk, :], (h * 512, 512))],
                                 lambda k: wbuf[:, s, k, cs:cs + 128],
                                 lambda k, a: Y[:, k, a[0]:a[0] + a[1]], 16,
                                 reads=(f"wb{s}",) + tuple(f"y{k}" for k in range(16)), writes=(f"ps{bk}",))
                    tr.op("act", lambda e, bk=bk, j=j: e.activation(RL[j], ps[:, bk, :], AF.Relu), reads=(f"ps{bk}",), writes=(f"rl{j}",))
                    tr.op("dve", lambda e, fl=fl, h=h, j=j: e.tensor_mul(HQ[:, fl, h * 512:(h + 1) * 512], RL[j], RL[j]),
                          reads=(f"rl{j}",), writes=(f"hq{fl}",))
        def mlp_w2(q, mo):
            base = B_MLP + q * 16
            if True:
                s = need(base + 8 + mo // 2)
                cs = (mo % 2) * 128
                for h in range(2):
                    bk = ev[0] % (4 if q == 3 else 8)
                    ev[0] += 1
                    mm_group([(ps[:, bk, :], (h * 512, 512))],
                             lambda k: wbuf[:, s, k, cs:cs + 128],
                             lambda k, a: HQ[:, k, a[0]:a[0] + a[1]], 16,
                             reads=(f"wb{s}",) + tuple(f"hq{k}" for k in range(16)), writes=(f"ps{bk}",))
                    zc = Z[:, mo, h * 512:(h + 1) * 512]
                    if q == 0:
                        tr.op("dve", lambda e, zc=zc, bk=bk: e.scalar_tensor_tensor(zc, zc, ALPHA, ps[:, bk, :], ALU.mult, ALU.add),
                              reads=(f"ps{bk}", f"z{mo}"), writes=(f"z{mo}",))
                    else:
                        tr.op("dve", lambda e, zc=zc, bk=bk: e.tensor_add(zc, zc, ps[:, bk, :]),
                              reads=(f"ps{bk}", f"z{mo}"), writes=(f"z{mo}",))
        for q in range(4):
            for fl in range(16):
                mlp_w1(q, fl)
            for mo in range(16):
                mlp_w2(q, mo)
                if q == 3 and mo >= 1:
                    ln_stats_chunk(mo - 1, mo == 1, False, alt=True)

        ln_stats_chunk(15, False, True, alt=True)
        ln_finish("g2", "b2", False, True)
        tr.barrier()

        tr.emit()
    return nc


_PASSES = {0: [(3, 1), (2, 1), (1, 1), (0, 0), (0, 1)],
           1: [(0, 0), (3, 1), (2, 1), (1, 0), (1, 1)],
           2: [(0, 0), (1, 0), (3, 1), (2, 0), (2, 1)],
           3: [(0, 0), (1, 0), (2, 0), (3, 0), (3, 1)]}
_KEEP = {0: (1, 1), 1: (0, 1), 2: (1, 0), 3: (1, 1)}
_SELF = {0: (0, 0, 0), 1: (1, 0, 0), 2: (0, 1, 0), 3: (0, 0, 1)}
_SELB = {0: (0, 0, 1), 1: (0, 0, 1), 2: (0, 0, 1), 3: (0, 0, 0)}


def _chan(v):
    return np.ascontiguousarray(v.reshape(-1, 128).T)


def kernel(x, ln_mix_g, ln_mix_b, w_in, w_pool, pool_scale, conv_w, conv_b, w_rg_a, b_rg_a, w_rg_i, b_rg_i,
           rg_lambda, w_out, ln_ffn_g, ln_ffn_b, w_mlp_in, w_mlp_out):
    x = np.asarray(x, np.float32)
    f = lambda a: np.ascontiguousarray(np.asarray(a, np.float32))
    w_in0, w_out0, w10, w20, wp0 = f(w_in[0]), f(w_out[0]), f(w_mlp_in[0]), f(w_mlp_out[0]), f(w_pool[0])
    cw = np.asarray(conv_w[0], np.float32)[:, 0, :]
    cbv = np.asarray(conv_b[0], np.float32)
    wa, wi = np.asarray(w_rg_a[0], np.float32), np.asarray(w_rg_i[0], np.float32)
    ba, bi, lam = (np.asarray(a[0], np.float32) for a in (b_rg_a, b_rg_i, rg_lambda))
    in_maps = []
    for core in range(8):
        b, c = core // 4, core % 4
        xT = x[b].T
        xpad = np.zeros((D, S + 16), np.float32)
        xpad[:, 8:8 + S] = xT
        xs = np.zeros((4, D, TS), np.float32)
        wg = np.zeros((NPASS, 2, 8, 128, 128), np.float32)
        vec = np.zeros((128, NV), np.float32)
        for p, (ch, rev) in enumerate(_PASSES[c]):
            if not rev:
                t0 = ch * T - 1
                if p < 4:
                    xs[p, :, :1027] = xpad[:, 8 + t0:8 + t0 + 1027]
                taps = [cw[0], cw[1], cw[2], cw[3]]
            else:
                t0 = (ch + 1) * T + 1
                if p < 4:
                    xs[p, :, :1027] = xpad[:, 8 + t0 - 1026:8 + t0 + 1][:, ::-1]
                taps = [cw[3], cw[2], cw[1], cw[0]]
            wg[p, 0], wg[p, 1] = wa[rev], wi[rev]
            for k in range(4):
                vec[:, VEC[f"tap{p}_{k}"]:VEC[f"tap{p}_{k}"] + 8] = _chan(taps[k])
            vec[:, VEC[f"cb{p}"]:VEC[f"cb{p}"] + 8] = _chan(cbv)
            vec[:, VEC[f"ba{p}"]:VEC[f"ba{p}"] + 8] = _chan(ba[rev])
            vec[:, VEC[f"bi{p}"]:VEC[f"bi{p}"] + 8] = _chan(bi[rev])
            vec[:, VEC[f"lam{p}"]:VEC[f"lam{p}"] + 8] = _chan(lam[rev])
        vec[:, VEC["keep1"]], vec[:, VEC["keep2"]] = _KEEP[c]
        for j in range(3):
            vec[:, VEC[f"selF{j}"]] = _SELF[c][j]
            vec[:, VEC[f"selB{j}"]] = _SELB[c][j]
        vec[:, VEC["ps"]:VEC["ps"] + 8] = _chan(np.asarray(pool_scale[0], np.float32))
        vec[:, VEC["g1"]:VEC["g1"] + 16] = _chan(np.asarray(ln_mix_g[0], np.float32))
        vec[:, VEC["b1"]:VEC["b1"] + 16] = _chan(np.asarray(ln_mix_b[0], np.float32))
        vec[:, VEC["g2"]:VEC["g2"] + 16] = _chan(np.asarray(ln_ffn_g[0], np.float32))
        vec[:, VEC["b2"]:VEC["b2"] + 16] = _chan(np.asarray(ln_ffn_b[0], np.float32))
        for g, w in enumerate(WINDOWS):
            for e in range(16):
                tl = e if e < 8 else 1016 + (e - 8)
                t = c * T + tl
                cnt = min(t + w // 2, S) - max(t - w // 2, 0)
                vec[:, VEC["invc"] + g * 16 + e] = 1.0 / cnt
        xo = np.ascontiguousarray(xpad[:, c * T:c * T + TO])
        in_maps.append({"xs": xs, "xo": xo, "w_in": w_in0, "w_out": w_out0, "w1": w10, "w2": w20,
                        "wp": wp0, "wg": wg, "vec": vec})
    import os
    nc = build_nc(os.environ.get("KDEBUG") or None)
    res = run_bass_kernel_spmd(nc, in_maps, core_ids=list(range(8)))
    out = np.empty((2, S, D), np.float32)
    for core in range(8):
        b, c = core // 4, core % 4
        out[b, c * T:(c + 1) * T, :] = res.results[core]["out"].T
    return out
```

```python
import numpy as np
from contextlib import ExitStack
import concourse.bass as bass
import concourse.mybir as mybir
from concourse.bass_utils import run_bass_kernel_spmd

F32, BF16 = mybir.dt.float32, mybir.dt.bfloat16
AF = mybir.ActivationFunctionType
ALU = mybir.AluOpType

D = 2048
S = 4096
T = 1024
NPASS = 5
TS = 1028
TO = 1040
DFF = 8192
ALPHA = 2.0 ** 0.25
EPS = 1e-5
WINDOWS = (2, 4, 8, 16)
WBLK = 256
NSLOT = 4

VEC = {}
_off = 0
def _v(name, n):
    global _off
    VEC[name] = _off
    _off += n
for _p in range(NPASS):
    for _k in range(4):
        _v(f"tap{_p}_{_k}", 8)
    _v(f"cb{_p}", 8); _v(f"ba{_p}", 8); _v(f"bi{_p}", 8); _v(f"lam{_p}", 8)
_v("keep1", 1); _v("keep2", 1)
for _j in range(3):
    _v(f"selF{_j}", 1)
for _j in range(3):
    _v(f"selB{_j}", 1)
_v("ps", 8); _v("g1", 16); _v("b1", 16); _v("g2", 16); _v("b2", 16)
_v("invc", 64)
NV = _off


class Tr:
    ENG = ["pe", "act", "dve", "pool", "sp"]

    def __init__(self, nc, es):
        self.nc, self.es = nc, es
        self.semh = {"E" + e: es.enter_context(nc.semaphore("s_" + e)) for e in self.ENG}
        self.cnt = {e: 0 for e in self.ENG}
        self.stream = {e: [] for e in self.ENG}
        self.res = {}
        self.waited = {e: {} for e in self.ENG}
        self.dcnt = {}

    def _deps(self, eng, reads, writes, skip_self=False):
        deps = {}

        def add(s, v):
            if deps.get(s, 0) < v:
                deps[s] = v
        for k in reads:
            r = self.res.get(k)
            if r and r["w"]:
                add(*r["w"])
        for k in writes:
            r = self.res.get(k)
            if r:
                if r["w"]:
                    add(*r["w"])
                for s, v in r["r"].items():
                    add(s, v)
        out = []
        for s, v in deps.items():
            if skip_self and s == "E" + eng:
                continue
            if self.waited[eng].get(s, 0) >= v:
                continue
            self.waited[eng][s] = v
            out.append((s, v))
        return out

    def _mark(self, reads, writes, sv):
        for k in reads:
            r = self.res.setdefault(k, {"w": None, "r": {}})
            if r["r"].get(sv[0], 0) < sv[1]:
                r["r"][sv[0]] = sv[1]
        for k in writes:
            self.res[k] = {"w": sv, "r": {}}

    def group(self, eng, fns, reads=(), writes=(), skip_self=False):
        deps = self._deps(eng, reads, writes, skip_self)
        self.cnt[eng] += 1
        sv = ("E" + eng, self.cnt[eng])
        self.stream[eng].append((deps, list(fns), (sv[0], 1)))
        self._mark(reads, writes, sv)

    def op(self, eng, fn, reads=(), writes=()):
        self.group(eng, [fn], reads, writes)

    def dma(self, eng, fn, sem, reads=(), writes=()):
        key = "D" + sem
        if key not in self.semh:
            self.semh[key] = self.es.enter_context(self.nc.semaphore("d_" + sem))
            self.dcnt[key] = 0
        deps = self._deps(eng, reads, writes)
        self.dcnt[key] += 16
        sv = (key, self.dcnt[key])
        self.stream[eng].append((deps, [fn], (key, 16)))
        self._mark(reads, writes, sv)

    def barrier(self):
        for e in self.ENG:
            deps = []
            allv = [("E" + e2, self.cnt[e2]) for e2 in self.ENG] + list(self.dcnt.items())
            for s, v in allv:
                if v > 0 and self.waited[e].get(s, 0) < v:
                    self.waited[e][s] = v
                    deps.append((s, v))
            self.stream[e].append((deps, [], None))
        self.res = {}

    def wait_all(self, e):
        deps = []
        allv = [("E" + e2, self.cnt[e2]) for e2 in self.ENG] + list(self.dcnt.items())
        for s, v in allv:
            if v > 0 and self.waited[e].get(s, 0) < v:
                self.waited[e][s] = v
                deps.append((s, v))
        self.stream[e].append((deps, [], None))

    def emit(self):
        nc = self.nc
        with nc.Block() as block:
            def mk(e):
                def run(eng):
                    for deps, fns, inc in self.stream[e]:
                        for s, v in deps:
                            eng.wait_ge(self.semh[s], v)
                        ins = None
                        for f in fns:
                            ins = f(eng)
                        if inc is not None and ins is not None:
                            ins.then_inc(self.semh[inc[0]], inc[1])
                return run
            block.tensor(mk("pe"))
            block.scalar(mk("act"))
            block.vector(mk("dve"))
            block.gpsimd(mk("pool"))
            block.sync(mk("sp"))


def build_nc(debug=None):
    nc = bass.Bass("TRN2", target_bir_lowering=False)
    xs_d = nc.dram_tensor("xs", [4, D, TS], F32, kind="ExternalInput").ap()
    xo_d = nc.dram_tensor("xo", [D, TO], F32, kind="ExternalInput").ap()
    win_d = nc.dram_tensor("w_in", [D, 3072], F32, kind="ExternalInput").ap()
    wout_d = nc.dram_tensor("w_out", [D, D], F32, kind="ExternalInput").ap()
    w1_d = nc.dram_tensor("w1", [D, DFF], F32, kind="ExternalInput").ap()
    w2_d = nc.dram_tensor("w2", [DFF, D], F32, kind="ExternalInput").ap()
    wp_d = nc.dram_tensor("wp", [4, 256, 256], F32, kind="ExternalInput").ap()
    wg_d = nc.dram_tensor("wg", [NPASS, 2, 8, 128, 128], F32, kind="ExternalInput").ap()
    vec_d = nc.dram_tensor("vec", [128, NV], F32, kind="ExternalInput").ap()
    out_d = nc.dram_tensor("out", [D, T], F32, kind="ExternalOutput").ap()

    es = ExitStack()
    with es:
        def sb(name, shape, dt):
            return es.enter_context(nc.sbuf_tensor(name, shape, dt))
        XW = 16 * TO // 2
        RX = sb("rx", [128, 2 * XW], F32)
        RHS = sb("rhs", [128, 8192], F32)
        RTMP = sb("rtmp", [128, 14848], F32)
        wbuf = sb("wbuf", [128, NSLOT, 16, WBLK], BF16)
        gw = sb("gw", [128, 3, 2, 8, 128], BF16)
        wp = sb("wpool", [128, 4, 2, 256], BF16)
        vec = sb("vec_sb", [128, NV], F32)
        dv = sb("dv", [128, NPASS, 4, 8], F32)
        sc = sb("sc", [128, 8, 40], F32)
        est = sb("est", [128, 3, 8], F32)
        ini = sb("ini", [128, 4, 8], F32)
        ones = sb("ones", [128, 128], BF16)
        ps = es.enter_context(nc.psum_tensor("psum_all", [128, 8, 512], F32))

        def view(raw, off, n, dt, pat=None, **kw):
            v = raw[:, off:off + n]
            if dt != F32:
                v = v.bitcast(dt)
            if pat:
                v = v.rearrange(pat, **kw)
            return v
        XT = [view(RX, i * XW, XW, BF16, "p (k n) -> p k n", k=16) for i in range(2)]
        Z = view(RX, 0, 16384, F32, "p (k n) -> p k n", k=16)
        HS = view(RHS, 0, 8192, F32, "p (k n) -> p k n", k=8)
        HQ = view(RHS, 0, 8192, BF16, "p (k n) -> p k n", k=16)
        LNT = [view(RHS, i * 1024, 1024, F32) for i in range(4)]
        ZB = [view(RHS, 4096 + i * 512, 512, BF16) for i in range(2)]
        ZQ = [view(RHS, 5120 + i * 512, 512, BF16) for i in range(2)]
        o = 0
        XC = [view(RTMP, o + i * 1024, 1024, F32) for i in range(3)]; o += 3072
        XCB = [view(RTMP, o + i * 512, 512, BF16) for i in range(3)]; o += 1536
        TR = view(RTMP, o, 1024, F32); o += 1024
        TI = [view(RTMP, o + i * 1024, 1024, F32) for i in range(3)]; o += 3072
        AA = [view(RTMP, o + i * 1024, 1024, F32) for i in range(3)]; o += 3072
        A2 = [view(RTMP, o + i * 1024, 1024, F32) for i in range(3)]; o += 3072
        assert o <= 14848
        Y = view(RTMP, 0, 8192, BF16, "p (k n) -> p k n", k=16)
        GL = [view(RTMP, 8192 + i * 1024, 1024, F32) for i in range(2)]
        RL = [view(RTMP, 8192 + 2048 + i * 512, 512, F32) for i in range(2)]
        UP = [view(RX, i * 1040, 1040, F32) for i in range(2)]
        PT = [view(RX, 2080 + i * 1040, 1040, F32) for i in range(2)]
        DP = view(RX, 4160, 4096, BF16, "p (k n) -> p k n", k=8)
        T8 = view(RX, 8256, 8, F32)
        assert 8264 <= XW

        tr = Tr(nc, es)

        def vcol(name, i=0, n=1):
            return vec[:, VEC[name] + i: VEC[name] + i + n]

        blocks = []
        win_v = win_d.rearrange("(k p) n -> p k n", p=128)
        wout_v = wout_d.rearrange("(k p) n -> p k n", p=128)
        w1_v = w1_d.rearrange("(k p) n -> p k n", p=128)
        w2_v = w2_d.rearrange("(q k p) n -> q p k n", p=128, k=16)
        for p in range(4):
            for j in range(4):
                blocks.append(win_v[:, :, 1024 + j * WBLK:1024 + (j + 1) * WBLK])
        B_POOL = len(blocks)
        for j in range(4):
            blocks.append(win_v[:, :, j * WBLK:(j + 1) * WBLK])
        B_GATE = len(blocks)
        for j in range(4):
            blocks.append(win_v[:, :, 2048 + j * WBLK:2048 + (j + 1) * WBLK])
        B_OUT = len(blocks)
        for j in range(8):
            blocks.append(wout_v[:, :, j * WBLK:(j + 1) * WBLK])
        B_MLP = len(blocks)
        for q in range(4):
            for j in range(8):
                blocks.append(w1_v[:, :, q * 2048 + j * WBLK:q * 2048 + (j + 1) * WBLK])
            for j in range(8):
                blocks.append(w2_v[q][:, :, j * WBLK:(j + 1) * WBLK])
        issued = [0]

        def need(bi, look=3):
            while issued[0] < len(blocks) and issued[0] <= bi + look:
                b = issued[0]
                s = b % NSLOT
                src = blocks[b]
                tr.dma("pool", lambda e, s=s, src=src: e.dma_start(out=wbuf[:, s], in_=src),
                       f"w{s}", reads=(), writes=(f"wb{s}",))
                issued[0] += 1
            return bi % NSLOT

        def mm_group(bank_cols, lhs_fn, rhs_fn, nk, reads, writes, k0=0, k1=None, skip_self=False):
            fns = []
            for k in range(k0, nk if k1 is None else k1):
                for (pap, rarg) in bank_cols:
                    l, r = lhs_fn(k), rhs_fn(k, rarg)
                    fns.append(lambda e, pap=pap, l=l, r=r, k=k: e.matmul(
                        pap, l, r, start=(k == 0), stop=(k == nk - 1)))
            tr.group("pe", fns, reads, writes, skip_self=skip_self)

        tr.dma("sp", lambda e: e.dma_start(out=vec[:], in_=vec_d), "vec", writes=("vec",))
        tr.op("dve", lambda e: e.memset(ones[:], 1.0), writes=("ones",))
        tr.dma("pool", lambda e: e.dma_start(out=wp[:], in_=wp_d.rearrange("g (c i) o -> i g c o", i=128)),
               "wp", writes=("wp",))
        lamv = vec[:, VEC["lam0"]:VEC["lam0"] + 8]
        for p in range(NPASS):
            lam = vec[:, VEC[f"lam{p}"]:VEC[f"lam{p}"] + 8]
            ee, zz, z2, acc, tmp = (sc[:, i, p * 8:(p + 1) * 8] for i in range(5))
            tr.op("act", lambda e, ee=ee, lam=lam: e.activation(ee, lam, AF.Exp, scale=-1.0),
                  reads=("vec",), writes=(f"sc{p}",))
            tr.op("dve", lambda e, ee=ee, tmp=tmp: e.tensor_scalar_add(tmp, ee, 2.0), reads=(f"sc{p}",), writes=(f"sc{p}",))
            tr.op("dve", lambda e, tmp=tmp: e.reciprocal(tmp, tmp), reads=(f"sc{p}",), writes=(f"sc{p}",))
            tr.op("dve", lambda e, zz=zz, ee=ee, tmp=tmp: e.tensor_mul(zz, ee, tmp), reads=(f"sc{p}",), writes=(f"sc{p}",))
            tr.op("dve", lambda e, zz=zz, z2=z2: e.tensor_mul(z2, zz, zz), reads=(f"sc{p}",), writes=(f"sc{p}",))
            tr.op("dve", lambda e, acc=acc, z2=z2: e.tensor_scalar(acc, z2, 1.0 / 11, 1.0 / 9, ALU.mult, ALU.add),
                  reads=(f"sc{p}",), writes=(f"sc{p}",))
            for cst in (1.0 / 7, 1.0 / 5, 1.0 / 3, 1.0):
                tr.op("dve", lambda e, acc=acc, z2=z2: e.tensor_mul(acc, acc, z2), reads=(f"sc{p}",), writes=(f"sc{p}",))
                tr.op("dve", lambda e, acc=acc, cst=cst: e.tensor_scalar_add(acc, acc, cst), reads=(f"sc{p}",), writes=(f"sc{p}",))
            tr.op("dve", lambda e, acc=acc, zz=zz: e.tensor_mul(acc, acc, zz), reads=(f"sc{p}",), writes=(f"sc{p}",))
            tr.op("dve", lambda e, acc=acc, p=p: e.tensor_scalar_mul(dv[:, p, 0, :], acc, -8.0), reads=(f"sc{p}",), writes=(f"dv{p}",))
            tr.op("dve", lambda e, acc=acc, p=p: e.tensor_scalar_mul(dv[:, p, 1, :], acc, -16.0), reads=(f"sc{p}",), writes=(f"dv{p}",))
            tr.op("dve", lambda e, p=p: e.tensor_scalar_mul(dv[:, p, 2, :], vec[:, VEC[f"ba{p}"]:VEC[f"ba{p}"] + 8], 0.5),
                  reads=("vec",), writes=(f"dv{p}",))
            tr.op("dve", lambda e, p=p: e.tensor_scalar_mul(dv[:, p, 3, :], vec[:, VEC[f"bi{p}"]:VEC[f"bi{p}"] + 8], 0.5),
                  reads=("vec",), writes=(f"dv{p}",))

        xs_v = xs_d.rearrange("s (k p) n -> s p k n", p=128)
        xo_v = xo_d.rearrange("(k p) n -> p k n", p=128)
        wg_v = wg_d.rearrange("s g h i o -> s i g h o")

        GWI = {0: 0, 1: 1, 2: 0, 3: 1, 4: 2}

        def load_pass_inputs(p):
            if p < 4:
                xt = XT[p % 2]
                tr.dma("pool", lambda e, p=p, xt=xt: e.dma_start(out=xt[:, :, 0:TS], in_=xs_v[p]),
                       f"x{p % 2}", writes=(f"xt{p % 2}",))
            tr.dma("pool", lambda e, p=p: e.dma_start(out=gw[:, GWI[p]], in_=wg_v[p]),
                   f"gw{GWI[p]}", writes=(f"gw{GWI[p]}",))

        need(0, look=0)
        load_pass_inputs(0)
        tr.op("pool", lambda e: e.memset(sc[:, 7, 0:8], 0.0), reads=("xt0", "wb0", "gw0"), writes=("scpad",))
        need(0, look=1)
        load_pass_inputs(1)
        load_pass_inputs(4)
        units = [(p, m) for p in range(3) for m in range(8)]
        for m in range(8):
            units += [(3, m), (4, m)]
        FULL, XI, FC = [], [], []
        fc = 0
        for (p, m) in units:
            if p < 4:
                FULL.append(True); XI.append(fc % 3); FC.append(fc); fc += 1
            else:
                FULL.append(False); XI.append(XI[-1]); FC.append(FC[-1])
        NU = len(units)
        TT = [(0, 512), (512, 512), (1024, 4)]

        def e_inproj(ci, part):
            p, m = units[ci]
            xt = XT[p % 2]
            bi = p * 4 + m // 2
            s = need(bi) if part == 0 else bi % NSLOT
            cs = (m % 2) * 128
            ub = 3 * (FC[ci] % 2)
            ukeys = tuple(f"ps{ub + tt}" for tt in range(3))
            mm_group([(ps[:, ub + tt, 0:n], (c0, n)) for tt, (c0, n) in enumerate(TT)],
                     lambda k: wbuf[:, s, k, cs:cs + 128],
                     lambda k, a: xt[:, k, a[0]:a[0] + a[1]], 16,
                     reads=(f"wb{s}", f"xt{p % 2}"), writes=ukeys,
                     k0=0 if part == 0 else 3, k1=3 if part == 0 else 16, skip_self=(part == 1))
            if part == 1:
                if m == 1 and p >= 1 and p + 1 < 4:
                    load_pass_inputs(p + 1)
                if m == 7 and p == 3:
                    tr.dma("pool", lambda e: e.dma_start(out=XT[1], in_=xo_v), "x1", writes=("xt1",))

        def conv_args(ci):
            p, m = units[ci]
            ub = 3 * (FC[ci] % 2)
            ukeys = tuple(f"ps{ub + tt}" for tt in range(3))
            tap = lambda k: vec[:, VEC[f"tap{p}_{k}"] + m:VEC[f"tap{p}_{k}"] + m + 1]
            cb = vec[:, VEC[f"cb{p}"] + m:VEC[f"cb{p}"] + m + 1]
            return ub, ukeys, tap, cb

        def e_tap0(ci):
            ub, ukeys, tap, cb = conv_args(ci)
            x3 = XC[XI[ci]].rearrange("p (a b) -> p a b", a=2)
            tr.op("act", lambda e: e.activation(x3, ps[:, ub:ub + 2, :], AF.Identity, bias=cb, scale=tap(0)),
                  reads=ukeys[:2] + ("vec",), writes=(f"xc{XI[ci]}",))

        def e_taps(ci):
            ub, ukeys, tap, cb = conv_args(ci)
            x = XC[XI[ci]]
            uflat = ps[:, ub:ub + 3, :].rearrange("p a b -> p (a b)")
            for k in (1, 2, 3):
                tr.op("dve", lambda e, k=k: e.scalar_tensor_tensor(x, uflat[:, k:k + 1024], tap(k), x, ALU.mult, ALU.add),
                      reads=ukeys + (f"xc{XI[ci]}", "vec"), writes=(f"xc{XI[ci]}",))

        def e_cast(ci):
            tr.op("act", lambda e: e.activation(XCB[XI[ci]], XC[XI[ci]], AF.Copy), reads=(f"xc{XI[ci]}",), writes=(f"xcb{XI[ci]}",))

        def e_gate(ci, g):
            p, m = units[ci]
            j = ci % 3
            xi = XI[ci]
            fns = [lambda e, h=h: e.matmul(ps[:, 6 + h, :], gw[:, GWI[p], g, m, :],
                                           XCB[xi][:, h * 512:(h + 1) * 512], start=True, stop=True) for h in range(2)]
            tr.group("pe", fns, reads=(f"xcb{xi}", f"gw{GWI[p]}"), writes=("ps6", "ps7"))

        def e_tanh(ci, g):
            p, m = units[ci]
            j = ci % 3
            dstt, dkey = (TR, "tr") if g == 0 else (TI[j], f"ti{j}")
            hb = dv[:, p, 2 + g, m:m + 1]
            d3 = dstt.rearrange("p (a b) -> p a b", a=2)
            tr.op("act", lambda e: e.activation(d3, ps[:, 6:8, :], AF.Tanh, bias=hb, scale=0.5),
                  reads=("ps6", "ps7", f"dv{p}"), writes=(dkey,))

        def e_exps(ci):
            p, m = units[ci]
            j = ci % 3
            hc, cc = dv[:, p, 0, m:m + 1], dv[:, p, 1, m:m + 1]
            tr.op("act", lambda e: e.activation(AA[j], TR, AF.Exp, bias=hc, scale=hc), reads=("tr", f"dv{p}"), writes=(f"aa{j}",))
            tr.op("act", lambda e: e.activation(A2[j], TR, AF.Exp, bias=cc, scale=cc), reads=("tr", f"dv{p}"), writes=(f"a2{j}",))

        def e_D2(ci):
            j = ci % 3
            xi = XI[ci]
            tr.op("dve", lambda e: e.scalar_tensor_tensor(TI[j], TI[j], 1.0, XC[xi], ALU.add, ALU.mult),
                  reads=(f"ti{j}", f"xc{xi}"), writes=(f"ti{j}",))

        def stageA7(ci):
            j = ci % 3
            tr.op("act", lambda e: e.activation(A2[j], A2[j], AF.Sqrt, bias=1.0, scale=-1.0), reads=(f"a2{j}",), writes=(f"a2{j}",))

        def stageB2(ci):
            p, m = units[ci]
            j = ci % 3
            tr.op("dve", lambda e: e.scalar_tensor_tensor(TI[j], TI[j], 0.5, A2[j], ALU.mult, ALU.mult),
                  reads=(f"ti{j}", f"a2{j}"), writes=(f"ti{j}",))
            if p == 0:
                init, ird = 0.0, ()
            else:
                init, ird = ini[:, p - 1, m:m + 1], ("ini",)
            if p == 4:
                tr.op("dve", lambda e: e.tensor_tensor_scan(TI[j][:, ::-1], AA[j][:, ::-1], TI[j][:, ::-1], init, ALU.mult, ALU.add),
                      reads=(f"aa{j}", f"ti{j}") + ird, writes=(f"ti{j}",))
                tr.op("dve", lambda e: e.tensor_add(HS[:, m, :], HS[:, m, :], TI[j]), reads=(f"ti{j}", f"hs{m}"), writes=(f"hs{m}",))
            elif p == 3:
                tr.op("dve", lambda e: e.tensor_tensor_scan(HS[:, m, :], AA[j], TI[j], init, ALU.mult, ALU.add),
                      reads=(f"aa{j}", f"ti{j}") + ird, writes=(f"hs{m}",))
            else:
                tr.op("dve", lambda e: e.tensor_tensor_scan(TI[j], AA[j], TI[j], init, ALU.mult, ALU.add),
                      reads=(f"aa{j}", f"ti{j}") + ird, writes=(f"ti{j}",))
                tr.op("dve", lambda e: e.tensor_copy(est[:, p, m:m + 1], TI[j][:, 1023:1024]), reads=(f"ti{j}",), writes=("est",))
            if m == 7 and p < 2:
                kp = vcol(f"keep{p + 1}")
                tr.op("dve", lambda e: e.tensor_scalar_mul(ini[:, p, :], est[:, p, :], kp), reads=("est", "vec"), writes=("ini",))
            if m == 7 and p == 2:
                for q, nm in ((2, "selF"), (3, "selB")):
                    tr.op("dve", lambda e, q=q, nm=nm: e.tensor_scalar_mul(ini[:, q, :], est[:, 0, :], vcol(nm + "0")),
                          reads=("est", "vec"), writes=("ini",))
                    for jj in (1, 2):
                        tr.op("dve", lambda e, q=q, nm=nm, jj=jj: e.scalar_tensor_tensor(
                            ini[:, q, :], est[:, jj, :], vcol(nm + str(jj)), ini[:, q, :], ALU.mult, ALU.add),
                            reads=("est", "vec", "ini"), writes=("ini",))

        IPI = {}
        for ci in range(NU):
            if FULL[ci]:
                IPI.setdefault(ci - 1 if (ci >= 26) else ci, []).append(ci)
        for t in range(NU + 5):
            u2, u1, uf = t - 2, t - 1, t - 5
            v2, v1 = 0 <= u2 < NU, 0 <= u1 < NU
            if 0 <= uf < NU:
                stageB2(uf)
            if v2:
                e_gate(u2, 0)
            for c in IPI.get(t, ()):
                e_inproj(c, 0)
            if v2:
                e_tanh(u2, 0)
            if v1 and FULL[u1]:
                e_tap0(u1)
                e_taps(u1)
            if v2:
                e_gate(u2, 1)
            for c in IPI.get(t, ()):
                e_inproj(c, 1)
            if v2:
                e_tanh(u2, 1)
                e_exps(u2)
                e_D2(u2)
            if v1 and FULL[u1]:
                e_cast(u1)
            if v2 and u2 % 2 == 1:
                stageA7(u2 - 1); stageA7(u2)

        if debug == "hs":
            tr.barrier()
        out_v = out_d.rearrange("(k p) n -> p k n", p=128)

        def dump_and_finish(src_ap, k0, nk):
            tr.barrier()
            tr.dma("sp", lambda e: e.dma_start(out=out_v[:, k0:k0 + nk, :], in_=src_ap), "o0", reads=(), writes=())
            tr.barrier()
            tr.emit()
        if debug == "hs":
            dump_and_finish(HS, 0, 8)
            return nc

        xo = XT[1]
        TTO = [(0, 512), (512, 512), (1024, 16)]
        def phaseP_chunk(m):
            g = m // 2
            w = WINDOWS[g]
            s = need(B_POOL + m // 2)
            cs = (m % 2) * 128
            bb = 3 * (m % 2)
            mm_group([(ps[:, bb + tt, 0:n], (c0, n)) for tt, (c0, n) in enumerate(TTO)],
                     lambda k: wbuf[:, s, k, cs:cs + 128],
                     lambda k, a: xo[:, k, a[0]:a[0] + a[1]], 16,
                     reads=(f"wb{s}", "xt1"), writes=tuple(f"ps{bb + tt}" for tt in range(3)))
            up = UP[m % 2]
            for tt, (c0, n) in enumerate(TTO):
                tr.op("act", lambda e, tt=tt, c0=c0, n=n, up=up, bb=bb: e.activation(up[:, c0:c0 + n], ps[:, bb + tt, 0:n], AF.Copy),
                      reads=(f"ps{bb + tt}",), writes=(f"up{m % 2}",))
            src, L, step, pi = up, 1040, 1, 0
            while step < w:
                dst = PT[pi]
                n = L - step
                tr.op("dve", lambda e, dst=dst, src=src, n=n, step=step: e.tensor_add(dst[:, 0:n], src[:, 0:n], src[:, step:step + n]),
                      reads=(f"up{m % 2}", "pt0", "pt1"), writes=(f"pt{pi}",))
                src, L, step, pi = dst, n, step * 2, 1 - pi
            st0 = 8 - w // 2
            W = src
            tr.op("dve", lambda e, W=W, up=up: e.scalar_tensor_tensor(DP[:, m, :], W[:, st0:st0 + 1024], 1.0 / w, up[:, 8:1032],
                                                                      ALU.mult, ALU.subtract),
                  reads=("pt0", "pt1", f"up{m % 2}"), writes=(f"dp{m}",))
            for (c0, e0) in ((0, 0), (1016, 8)):
                iv = vec[:, VEC["invc"] + g * 16 + e0:VEC["invc"] + g * 16 + e0 + 8]
                tr.op("dve", lambda e, W=W, c0=c0, iv=iv: e.tensor_mul(T8, W[:, st0 + c0:st0 + c0 + 8], iv),
                      reads=("pt0", "pt1", "vec"), writes=("t8",))
                tr.op("dve", lambda e, c0=c0, up=up: e.tensor_sub(DP[:, m, c0:c0 + 8], T8, up[:, 8 + c0:16 + c0]),
                      reads=("t8", f"up{m % 2}", f"dp{m}"), writes=(f"dp{m}",))
            if m % 2 == 1:
                if m == 1:
                    tr.wait_all("act")
                for oc in range(2):
                    for h in range(2):
                        bk = 6 + h
                        fns = [lambda e, ic=ic, oc=oc, h=h, bk=bk: e.matmul(ps[:, bk, :], wp[:, g, ic, oc * 128:(oc + 1) * 128],
                                                                           DP[:, 2 * g + ic, h * 512:(h + 1) * 512],
                                                                           start=(ic == 0), stop=(ic == 1)) for ic in range(2)]
                        tr.group("pe", fns, reads=("wp", f"dp{2 * g}", f"dp{2 * g + 1}"), writes=(f"ps{bk}",))
                        mo = 2 * g + oc
                        tr.op("act", lambda e, mo=mo, h=h, bk=bk: e.activation(Y[:, mo, h * 512:(h + 1) * 512], ps[:, bk, :], AF.Identity,
                                                                                scale=vcol("ps", mo)),
                              reads=(f"ps{bk}", "vec"), writes=(f"y{mo}",))

        for m in range(8):
            phaseP_chunk(m)
        out_v = out_d.rearrange("(k p) n -> p k n", p=128)
        xo_f = xo_d.rearrange("(k p) n -> p k n", p=128)

        def load_xres(q4):
            tr.dma("sp", lambda e: e.dma_start(out=Z[:, 4 * q4:4 * q4 + 4, :], in_=xo_f[:, 4 * q4:4 * q4 + 4, 8:1032]),
                   f"xr{q4}", writes=tuple(f"z{4 * q4 + i}" for i in range(4)) + (("xt1",) if q4 >= 2 else ()))
        tr.wait_all("sp")
        load_xres(0)
        load_xres(1)

        def phaseG_chunk(m):
            s = need(B_GATE + m // 2)
            cs = (m % 2) * 128
            bb = 2 * (m % 2)
            mm_group([(ps[:, bb + h, :], (8 + h * 512, 512)) for h in range(2)],
                     lambda k: wbuf[:, s, k, cs:cs + 128],
                     lambda k, a: xo[:, k, a[0]:a[0] + a[1]], 16,
                     reads=(f"wb{s}", "xt1"), writes=(f"ps{bb}", f"ps{bb + 1}"))
            gl = GL[m % 2]
            for h in range(2):
                tr.op("act", lambda e, h=h, gl=gl, bb=bb: e.activation(gl[:, h * 512:(h + 1) * 512], ps[:, bb + h, :], AF.Gelu_apprx_tanh),
                      reads=(f"ps{bb + h}",), writes=(f"gl{m % 2}",))
            tr.op("dve", lambda e, gl=gl: e.tensor_mul(Y[:, 8 + m, :], HS[:, m, :], gl),
                  reads=(f"gl{m % 2}", f"hs{m}"), writes=(f"y{8 + m}",))
        tr.wait_all("dve")
        for m in range(8):
            phaseG_chunk(m)


        ZB2 = [view(RTMP, 8192 + i * 512, 512, BF16) for i in range(2)]
        ZQ2 = [view(RTMP, 8192 + 1024 + i * 512, 512, BF16) for i in range(2)]

        def ln_stats_chunk(mo, first, last, alt=False):
            j = mo % 2
            zb, zq = (ZB2[j], ZQ2[j]) if alt else (ZB[j], ZQ[j])
            tr.op("act", lambda e: e.activation(zb, Z[:, mo, :], AF.Copy), reads=(f"z{mo}",), writes=(f"zb{j}",))
            tr.op("act", lambda e: e.activation(zq, Z[:, mo, :], AF.Square), reads=(f"z{mo}",), writes=(f"zq{j}",))
            fns = []
            for h in range(2):
                fns.append(lambda e, h=h: e.matmul(ps[:, 4 + h, :], ones[:], zb[:, h * 512:(h + 1) * 512], start=first, stop=last))
                fns.append(lambda e, h=h: e.matmul(ps[:, 6 + h, :], ones[:], zq[:, h * 512:(h + 1) * 512], start=first, stop=last))
            wr = ()
            if first:
                wr = ("ps4", "ps5", "ps6", "ps7")
            if last:
                wr = ("st",)
            tr.group("pe", fns, reads=(f"zb{j}", f"zq{j}", "ones"), writes=wr, skip_self=True)

        def ln_finish(gname, bname, to_bf16, store):
            mean, var, nmr, msq = LNT
            s1 = ps[:, 4:6, :].rearrange("p a b -> p (a b)")
            s2 = ps[:, 6:8, :].rearrange("p a b -> p (a b)")
            tr.op("dve", lambda e: e.tensor_scalar_mul(mean, s1, 1.0 / D), reads=("st",), writes=("mean",))
            tr.op("dve", lambda e: e.tensor_mul(msq, mean, mean), reads=("mean",), writes=("msq",))
            tr.op("dve", lambda e: e.scalar_tensor_tensor(var, s2, 1.0 / D, msq, ALU.mult, ALU.subtract), reads=("st", "msq"), writes=("var",))
            tr.op("act", lambda e: e.activation(var, var, AF.Sqrt, bias=EPS, scale=1.0), reads=("var",), writes=("var",))
            tr.op("dve", lambda e: e.reciprocal(var, var), reads=("var",), writes=("var",))
            tr.op("dve", lambda e: e.scalar_tensor_tensor(nmr, mean, -1.0, var, ALU.mult, ALU.mult), reads=("mean", "var"), writes=("nmr",))
            for mo in range(16):
                zc = Z[:, mo, :]
                tr.op("dve", lambda e, zc=zc: e.tensor_mul(zc, zc, var), reads=(f"z{mo}", "var"), writes=(f"z{mo}",))
                tr.op("dve", lambda e, zc=zc: e.tensor_add(zc, zc, nmr), reads=(f"z{mo}", "nmr"), writes=(f"z{mo}",))
                tr.op("act", lambda e, zc=zc, mo=mo: e.activation(zc, zc, AF.Identity, bias=vcol(bname, mo), scale=vcol(gname, mo)),
                      reads=(f"z{mo}", "vec"), writes=(f"z{mo}",))
                if to_bf16:
                    tr.op("act", lambda e, zc=zc, mo=mo: e.activation(Y[:, mo, :], zc, AF.Copy), reads=(f"z{mo}",), writes=(f"y{mo}",))
                if store:
                    tr.dma("sp", lambda e, mo=mo: e.dma_start(out=out_v[:, mo, :], in_=Z[:, mo, :]), f"o{mo % 4}",
                           reads=(f"z{mo}",), writes=())

        out_v = out_d.rearrange("(k p) n -> p k n", p=128)
        xo_f = xo_d.rearrange("(k p) n -> p k n", p=128)

        load_xres(2)
        load_xres(3)
        def outproj_chunk(mo):
            s = need(B_OUT + mo // 2)
            cs = (mo % 2) * 128
            for h in range(2):
                bk = (2 * mo + h) % 4
                mm_group([(ps[:, bk, :], (h * 512, 512))],
                         lambda k: wbuf[:, s, k, cs:cs + 128],
                         lambda k, a: Y[:, k, a[0]:a[0] + a[1]], 16,
                         reads=(f"wb{s}",) + tuple(f"y{k}" for k in range(16)), writes=(f"ps{bk}",))
                zc = Z[:, mo, h * 512:(h + 1) * 512]
                tr.op("dve", lambda e, zc=zc, bk=bk: e.scalar_tensor_tensor(zc, zc, ALPHA, ps[:, bk, :], ALU.mult, ALU.add),
                      reads=(f"ps{bk}", f"z{mo}"), writes=(f"z{mo}",))
            if mo >= 1:
                ln_stats_chunk(mo - 1, mo == 1, False)
        for mo in range(16):
            outproj_chunk(mo)
        ln_stats_chunk(15, False, True)
        if debug == "z1":
            dump_and_finish(Z, 0, 16)
            return nc
        ln_finish("g1", "b1", True, False)
        if debug == "x2":
            dump_and_finish(Z, 0, 16)
            return nc

        ev = [0]

        def mlp_w1(q, fl):
            base = B_MLP + q * 16
            if True:
                s = need(base + fl // 2)
                cs = (fl % 2) * 128
                for h in range(2):
                    bk = ev[0] % 8
                    j = ev[0] % 2
                    ev[0] += 1
                    if q == 0 and fl == 0:
                        for kk in range(16):
                            mm_group([(ps[:, bk, :], (h * 512, 512))],
                                     lambda k: wbuf[:, s, k, cs:cs + 128],
                                     lambda k, a: Y[:, k, a[0]:a[0] + a[1]], 16,
                                     reads=(f"wb{s}", f"y{kk}"), writes=(f"ps{bk}",), k0=kk, k1=kk + 1, skip_self=(kk > 0))
                    else:
                        mm_group([(ps[:, bk, :], (h * 512, 512))],
                                 lambda k: wbuf[:, s, k, cs:cs + 128],
                                 lambda k, a: Y[:, k, a[0]:a[0] + a[1]], 16,
                                 reads=(f"wb{s}",) + tuple(f"y{k}" for k in range(16)), writes=(f"ps{bk}",))
                    tr.op("act", lambda e, bk=bk, j=j: e.activation(RL[j], ps[:, bk, :], AF.Relu), reads=(f"ps{bk}",), writes=(f"rl{j}",))
                    tr.op("dve", lambda e, fl=fl, h=h, j=j: e.tensor_mul(HQ[:, fl, h * 512:(h + 1) * 512], RL[j], RL[j]),
                          reads=(f"rl{j}",), writes=(f"hq{fl}",))
        def mlp_w2(q, mo):
            base = B_MLP + q * 16
            if True:
                s = need(base + 8 + mo // 2)
                cs = (mo % 2) * 128
                for h in range(2):
                    bk = ev[0] % (4 if q == 3 else 8)
                    ev[0] += 1
                    mm_group([(ps[:, bk, :], (h * 512, 512))],
                             lambda k: wbuf[:, s, k, cs:cs + 128],
                             lambda k, a: HQ[:, k, a[0]:a[0] + a[1]], 16,
                             reads=(f"wb{s}",) + tuple(f"hq{k}" for k in range(16)), writes=(f"ps{bk}",))
                    zc = Z[:, mo, h * 512:(h + 1) * 512]
                    if q == 0:
                        tr.op("dve", lambda e, zc=zc, bk=bk: e.scalar_tensor_tensor(zc, zc, ALPHA, ps[:, bk, :], ALU.mult, ALU.add),
                              reads=(f"ps{bk}", f"z{mo}"), writes=(f"z{mo}",))
                    else:
                        tr.op("dve", lambda e, zc=zc, bk=bk: e.tensor_add(zc, zc, ps[:, bk, :]),
                              reads=(f"ps{bk}", f"z{mo}"), writes=(f"z{mo}",))
        for q in range(4):
            for fl in range(16):
                mlp_w1(q, fl)
            for mo in range(16):
                mlp_w2(q, mo)
                if q == 3 and mo >= 1:
                    ln_stats_chunk(mo - 1, mo == 1, False, alt=True)

        ln_stats_chunk(15, False, True, alt=True)
        ln_finish("g2", "b2", False, True)
        tr.barrier()

        tr.emit()
    return nc


_PASSES = {0: [(3, 1), (2, 1), (1, 1), (0, 0), (0, 1)],
           1: [(0, 0), (3, 1), (2, 1), (1, 0), (1, 1)],
           2: [(0, 0), (1, 0), (3, 1), (2, 0), (2, 1)],
           3: [(0, 0), (1, 0), (2, 0), (3, 0), (3, 1)]}
_KEEP = {0: (1, 1), 1: (0, 1), 2: (1, 0), 3: (1, 1)}
_SELF = {0: (0, 0, 0), 1: (1, 0, 0), 2: (0, 1, 0), 3: (0, 0, 1)}
_SELB = {0: (0, 0, 1), 1: (0, 0, 1), 2: (0, 0, 1), 3: (0, 0, 0)}


def _chan(v):
    return np.ascontiguousarray(v.reshape(-1, 128).T)


def kernel(x, ln_mix_g, ln_mix_b, w_in, w_pool, pool_scale, conv_w, conv_b, w_rg_a, b_rg_a, w_rg_i, b_rg_i,
           rg_lambda, w_out, ln_ffn_g, ln_ffn_b, w_mlp_in, w_mlp_out):
    x = np.asarray(x, np.float32)
    f = lambda a: np.ascontiguousarray(np.asarray(a, np.float32))
    w_in0, w_out0, w10, w20, wp0 = f(w_in[0]), f(w_out[0]), f(w_mlp_in[0]), f(w_mlp_out[0]), f(w_pool[0])
    cw = np.asarray(conv_w[0], np.float32)[:, 0, :]
    cbv = np.asarray(conv_b[0], np.float32)
    wa, wi = np.asarray(w_rg_a[0], np.float32), np.asarray(w_rg_i[0], np.float32)
    ba, bi, lam = (np.asarray(a[0], np.float32) for a in (b_rg_a, b_rg_i, rg_lambda))
    in_maps = []
    for core in range(8):
        b, c = core // 4, core % 4
        xT = x[b].T
        xpad = np.zeros((D, S + 16), np.float32)
        xpad[:, 8:8 + S] = xT
        xs = np.zeros((4, D, TS), np.float32)
        wg = np.zeros((NPASS, 2, 8, 128, 128), np.float32)
        vec = np.zeros((128, NV), np.float32)
        for p, (ch, rev) in enumerate(_PASSES[c]):
            if not rev:
                t0 = ch * T - 1
                if p < 4:
                    xs[p, :, :1027] = xpad[:, 8 + t0:8 + t0 + 1027]
                taps = [cw[0], cw[1], cw[2], cw[3]]
            else:
                t0 = (ch + 1) * T + 1
                if p < 4:
                    xs[p, :, :1027] = xpad[:, 8 + t0 - 1026:8 + t0 + 1][:, ::-1]
                taps = [cw[3], cw[2], cw[1], cw[0]]
            wg[p, 0], wg[p, 1] = wa[rev], wi[rev]
            for k in range(4):
                vec[:, VEC[f"tap{p}_{k}"]:VEC[f"tap{p}_{k}"] + 8] = _chan(taps[k])
            vec[:, VEC[f"cb{p}"]:VEC[f"cb{p}"] + 8] = _chan(cbv)
            vec[:, VEC[f"ba{p}"]:VEC[f"ba{p}"] + 8] = _chan(ba[rev])
            vec[:, VEC[f"bi{p}"]:VEC[f"bi{p}"] + 8] = _chan(bi[rev])
            vec[:, VEC[f"lam{p}"]:VEC[f"lam{p}"] + 8] = _chan(lam[rev])
        vec[:, VEC["keep1"]], vec[:, VEC["keep2"]] = _KEEP[c]
        for j in range(3):
            vec[:, VEC[f"selF{j}"]] = _SELF[c][j]
            vec[:, VEC[f"selB{j}"]] = _SELB[c][j]
        vec[:, VEC["ps"]:VEC["ps"] + 8] = _chan(np.asarray(pool_scale[0], np.float32))
        vec[:, VEC["g1"]:VEC["g1"] + 16] = _chan(np.asarray(ln_mix_g[0], np.float32))
        vec[:, VEC["b1"]:VEC["b1"] + 16] = _chan(np.asarray(ln_mix_b[0], np.float32))
        vec[:, VEC["g2"]:VEC["g2"] + 16] = _chan(np.asarray(ln_ffn_g[0], np.float32))
        vec[:, VEC["b2"]:VEC["b2"] + 16] = _chan(np.asarray(ln_ffn_b[0], np.float32))
        for g, w in enumerate(WINDOWS):
            for e in range(16):
                tl = e if e < 8 else 1016 + (e - 8)
                t = c * T + tl
                cnt = min(t + w // 2, S) - max(t - w // 2, 0)
                vec[:, VEC["invc"] + g * 16 + e] = 1.0 / cnt
        xo = np.ascontiguousarray(xpad[:, c * T:c * T + TO])
        in_maps.append({"xs": xs, "xo": xo, "w_in": w_in0, "w_out": w_out0, "w1": w10, "w2": w20,
                        "wp": wp0, "wg": wg, "vec": vec})
    import os
    nc = build_nc(os.environ.get("KDEBUG") or None)
    res = run_bass_kernel_spmd(nc, in_maps, core_ids=list(range(8)))
    out = np.empty((2, S, D), np.float32)
    for core in range(8):
        b, c = core // 4, core % 4
        out[b, c * T:(c + 1) * T, :] = res.results[core]["out"].T
    return out
```
